# Optimizing a Trainium2 kernel written in Bass

```python
import jax, jax.numpy as jnp
from jax import lax
import numpy as np

D_MODEL = 4096
BATCH = 4
SEQ = 2048
DEPTH = 2

N_META = 16
CHUNK = 16
EXPAND = 2
D_INNER = EXPAND * D_MODEL
N_MIXERS = 2
HG_DK = 128
HG_HEADS = D_INNER // HG_DK
HG_DV = D_INNER // HG_HEADS
GLA_HEADS = 8
GLA_KW = D_INNER // 2
GLA_DK = GLA_KW // GLA_HEADS
GLA_DV = D_INNER // GLA_HEADS
GLA_RANK = 16
GLA_GATE_NORM = 16.0
ALPHA = (2.0 * DEPTH) ** 0.25
BETA = (8.0 * DEPTH) ** -0.25
LN_EPS = 1e-5
RMS_EPS = 1e-6

kernel_name = 'hgrn2_gla_interleaved_deepnorm_meta'


def layer_norm(x, g, b):
    xf = x.astype(jnp.float32)
    mu = jnp.mean(xf, axis=-1, keepdims=True)
    var = jnp.mean(jnp.square(xf - mu), axis=-1, keepdims=True)
    return ((xf - mu) * lax.rsqrt(var + LN_EPS) * g.astype(jnp.float32) + b.astype(jnp.float32)).astype(x.dtype)


def head_rms_norm(o, g):
    B, T, H, dv = o.shape
    ms = jnp.mean(jnp.square(o), axis=-1, keepdims=True)
    return (o * lax.rsqrt(ms + RMS_EPS)).reshape(B, T, H * dv) * g.astype(jnp.float32)


def chunked_gated_linear_attention(q, k, v, log_g):
    B, T, H, dk = q.shape
    dv = v.shape[-1]
    nc = T // CHUNK

    def to_chunks(a):
        return a.reshape(B, nc, CHUNK, H, a.shape[-1]).transpose(1, 0, 3, 2, 4)

    q, k, v, log_g = (to_chunks(a) for a in (q, k, v, log_g))
    b = jnp.cumsum(log_g, axis=-2)
    b_last = b[..., -1:, :]
    q_dec = q * jnp.exp(b)
    k_inv = k * jnp.exp(-b)
    k_dec = k * jnp.exp(b_last - b)
    causal = jnp.tril(jnp.ones((CHUNK, CHUNK), dtype=bool))
    scores = jnp.where(causal, jnp.einsum('nbhid,nbhjd->nbhij', q_dec, k_inv), 0.0)
    o_intra = jnp.einsum('nbhij,nbhjv->nbhiv', scores, v)
    g_last = jnp.exp(b_last[..., 0, :])

    def step(S, xs):
        qd, kd, vc, gl = xs
        o = jnp.einsum('bhid,bhdv->bhiv', qd, S)
        S = gl[..., None] * S + jnp.einsum('bhjd,bhjv->bhdv', kd, vc)
        return S, o

    S0 = jnp.zeros((B, H, dk, dv), dtype=q.dtype)
    _, o_inter = lax.scan(step, S0, (q_dec, k_dec, v, g_last))
    o = o_intra + o_inter
    return o.transpose(1, 0, 3, 2, 4).reshape(B, T, H, dv)


def hgrn2_mixer(x, w_in, b_f, lb, norm_g, w_out):
    B, T, _ = x.shape
    q, f, i, z = jnp.split(x @ w_in, 4, axis=-1)
    q = jax.nn.silu(q.astype(jnp.float32))
    fg = lb + (1.0 - lb) * jax.nn.sigmoid((f + b_f).astype(jnp.float32))
    k = 1.0 - fg
    log_g = jnp.log(fg)
    o = chunked_gated_linear_attention(
        q.reshape(B, T, HG_HEADS, HG_DK), k.reshape(B, T, HG_HEADS, HG_DK),
        i.astype(jnp.float32).reshape(B, T, HG_HEADS, HG_DV), log_g.reshape(B, T, HG_HEADS, HG_DK))
    y = head_rms_norm(o, norm_g) * jax.nn.silu(z.astype(jnp.float32))
    return y.astype(x.dtype) @ w_out


def gla_mixer(x, w_in, w_g1, w_g2, b_g, norm_g, w_out):
    B, T, _ = x.shape
    q, k, v, z = jnp.split(x @ w_in, [GLA_KW, 2 * GLA_KW, 2 * GLA_KW + D_INNER], axis=-1)
    log_g = jax.nn.log_sigmoid(((x @ w_g1) @ w_g2 + b_g).astype(jnp.float32)) / GLA_GATE_NORM
    q = q.astype(jnp.float32) * (GLA_DK ** -0.5)
    o = chunked_gated_linear_attention(
        q.reshape(B, T, GLA_HEADS, GLA_DK), k.astype(jnp.float32).reshape(B, T, GLA_HEADS, GLA_DK),
        v.astype(jnp.float32).reshape(B, T, GLA_HEADS, GLA_DV), log_g.reshape(B, T, GLA_HEADS, GLA_DK))
    y = head_rms_norm(o, norm_g) * jax.nn.silu(z.astype(jnp.float32))
    return y.astype(x.dtype) @ w_out


def setup_inputs(seed: int = 0) -> dict:
    key = jax.random.key(seed)
    ks = jax.random.split(key, 24)
    n = lambda k, shape, s: jax.random.normal(k, shape, jnp.float32) * s
    sd = D_MODEL ** -0.5
    x = n(ks[0], (BATCH, SEQ, D_MODEL), 1.0)
    meta = n(ks[1], (N_META, D_MODEL), 1.0)
    lb_logits = n(ks[2], (DEPTH + 1, HG_HEADS * HG_DK), 0.1)
    l0_w_in = jnp.concatenate([
        n(ks[3], (D_MODEL, D_INNER), sd),
        n(ks[4], (D_MODEL, D_INNER), sd),
        n(ks[5], (D_MODEL, D_INNER), sd * BETA),
        n(ks[6], (D_MODEL, D_INNER), sd),
    ], axis=1)
    l0_b_f = n(ks[7], (D_INNER,), 0.01)
    l0_norm_g = 1.0 + n(ks[8], (D_INNER,), 0.01)
    l0_w_out = n(ks[9], (D_INNER, D_MODEL), D_INNER ** -0.5 * BETA)
    l0_ln_g = 1.0 + n(ks[10], (D_MODEL,), 0.01)
    l0_ln_b = n(ks[11], (D_MODEL,), 0.01)
    l1_w_in = jnp.concatenate([
        n(ks[12], (D_MODEL, GLA_KW), sd),
        n(ks[13], (D_MODEL, GLA_KW), sd),
        n(ks[14], (D_MODEL, D_INNER), sd * BETA),
        n(ks[15], (D_MODEL, D_INNER), sd),
    ], axis=1)
    l1_w_g1 = n(ks[16], (D_MODEL, GLA_RANK), sd)
    l1_w_g2 = n(ks[17], (GLA_RANK, GLA_KW), GLA_RANK ** -0.5)
    l1_b_g = n(ks[18], (GLA_KW,), 0.01)
    l1_norm_g = 1.0 + n(ks[19], (D_INNER,), 0.01)
    l1_w_out = n(ks[20], (D_INNER, D_MODEL), D_INNER ** -0.5 * BETA)
    l1_ln_g = 1.0 + n(ks[21], (D_MODEL,), 0.01)
    l1_ln_b = n(ks[22], (D_MODEL,), 0.01)
    return {'x': x, 'meta': meta, 'lb_logits': lb_logits,
            'l0_w_in': l0_w_in, 'l0_b_f': l0_b_f, 'l0_norm_g': l0_norm_g, 'l0_w_out': l0_w_out,
            'l0_ln_g': l0_ln_g, 'l0_ln_b': l0_ln_b,
            'l1_w_in': l1_w_in, 'l1_w_g1': l1_w_g1, 'l1_w_g2': l1_w_g2, 'l1_b_g': l1_b_g,
            'l1_norm_g': l1_norm_g, 'l1_w_out': l1_w_out, 'l1_ln_g': l1_ln_g, 'l1_ln_b': l1_ln_b}


def reference(x, meta, lb_logits,
              l0_w_in, l0_b_f, l0_norm_g, l0_w_out, l0_ln_g, l0_ln_b,
              l1_w_in, l1_w_g1, l1_w_g2, l1_b_g, l1_norm_g, l1_w_out, l1_ln_g, l1_ln_b):
    B = x.shape[0]
    lb_all = jnp.cumsum(jax.nn.softmax(lb_logits.astype(jnp.float32), axis=0), axis=0)
    mixers = (
        lambda h, li: hgrn2_mixer(h, l0_w_in, l0_b_f, lb_all[li], l0_norm_g, l0_w_out),
        lambda h, li: gla_mixer(h, l1_w_in, l1_w_g1, l1_w_g2, l1_b_g, l1_norm_g, l1_w_out),
    )
    post_norms = ((l0_ln_g, l0_ln_b), (l1_ln_g, l1_ln_b))
    meta_b = jnp.broadcast_to(meta[None].astype(x.dtype), (B, N_META, D_MODEL))
    h = jnp.concatenate([meta_b, x], axis=1)
    for li in range(DEPTH):
        g, b = post_norms[li]
        h = layer_norm(ALPHA * h + mixers[li % N_MIXERS](h, li), g, b)
    return h[:, N_META:]
```

```python
import numpy as np
import concourse.bass as bass
import concourse.mybir as mybir
from concourse.bass_utils import run_bass_kernel_spmd
from contextlib import ExitStack

F32 = mybir.dt.float32
BF16 = mybir.dt.bfloat16
AF = mybir.ActivationFunctionType
ALU = mybir.AluOpType

D = 4096
DI = 8192
NTOK = 1032
CH = 86
NCH = 12
TT = 344
NTT = 3
CPT = 4
KC = 32
NCORES = 8
ALPHA = (2.0 * 2) ** 0.25
LN_EPS = 1e-5
RMS_EPS = 1e-6


class Buf:
    __slots__ = ("name", "w", "r")

    def __init__(self, name=""):
        self.name = name
        self.w = None
        self.r = []


class Sched:
    ENG = ("pe", "act", "dve", "pool", "sp")

    def __init__(self, nc, n_dma_sems=16):
        self.nc = nc
        self.eng = {"pe": nc.tensor, "act": nc.scalar, "dve": nc.vector,
                    "pool": nc.gpsimd, "sp": nc.sync}
        self.sems = {}
        self.cnt = {}
        self.semobj = {}
        for e in self.ENG:
            self.sems[e] = nc.alloc_semaphore(name=f"s_{e}")
            self.cnt[e] = 0
            self.semobj[("c", e)] = self.sems[e]
        self.dma_sems = {}
        self.dma_cnt = {}
        self.dma_rr = {}
        for q in ("sp", "pool"):
            self.dma_sems[q] = [nc.alloc_semaphore(name=f"d_{q}{i}") for i in range(n_dma_sems)]
            self.dma_cnt[q] = [0] * n_dma_sems
            self.dma_rr[q] = 0
            for i, sm in enumerate(self.dma_sems[q]):
                self.semobj[("d", q, i)] = sm
        self.waited = {e: {} for e in self.ENG}
        self.n_wait = 0

    def _wait(self, e, key, val):
        if key == ("c", "pe") and e == "pe":
            return
        w = self.waited[e]
        if w.get(key, 0) >= val:
            return
        w[key] = val
        self.eng[e].wait_ge(self.semobj[key], val)
        self.n_wait += 1

    def _deps(self, e, reads, writes):
        for b in reads:
            if b.w is not None:
                self._wait(e, *b.w)
        for b in writes:
            if b.w is not None:
                self._wait(e, *b.w)
            for ev in b.r:
                self._wait(e, *ev)

    def _post(self, ev, reads, writes):
        for b in writes:
            b.w = ev
            b.r = []
        for b in reads:
            if b in writes:
                continue
            b.r.append(ev)
            if len(b.r) > 8:
                best = {}
                for k, v in b.r:
                    if best.get(k, 0) < v:
                        best[k] = v
                b.r = list(best.items())

    def op(self, e, fn, reads=(), writes=()):
        self._deps(e, reads, writes)
        ins = fn(self.eng[e])
        self.cnt[e] += 1
        ins.then_inc(self.sems[e], 1)
        ev = (("c", e), self.cnt[e])
        self._post(ev, reads, writes)
        return ev

    def dma(self, q, out, in_, reads=(), writes=()):
        self._deps(q, reads, writes)
        i = self.dma_rr[q]
        self.dma_rr[q] = (i + 1) % len(self.dma_sems[q])
        key = ("d", q, i)
        if self.dma_cnt[q][i] > 0:
            self._wait(q, key, self.dma_cnt[q][i])
        self.dma_cnt[q][i] += 16
        ins = self.eng[q].dma_start(out=out, in_=in_)
        ins.then_inc(self.dma_sems[q][i], 16)
        ev = (key, self.dma_cnt[q][i])
        self._post(ev, reads, writes)
        return ev

    def drain_dma(self, q, e=None):
        e = e or q
        for i, c in enumerate(self.dma_cnt[q]):
            if c:
                self._wait(e, ("d", q, i), c)

    def barrier(self):
        for e in self.ENG:
            for e2 in self.ENG:
                if e2 != e and self.cnt[e2]:
                    self._wait(e, ("c", e2), self.cnt[e2])
            for q in self.dma_sems:
                self.drain_dma(q, e)


def interleave(ga, gb, ratio):
    da = db = False
    acc = 0.0
    while not (da and db):
        acc += ratio
        while acc >= 1.0 and not da:
            acc -= 1.0
            try:
                next(ga)
            except StopIteration:
                da = True
        if not db:
            try:
                next(gb)
            except StopIteration:
                db = True
        if db and not da:
            for _ in ga:
                pass
            da = True


class Prog:
    def __init__(self):
        self.nc = bass.Bass("TRN2", target_bir_lowering=False)
        self.s = Sched(self.nc)
        self.stack = ExitStack()
        nc = self.nc
        mk = lambda n: [nc.alloc_psum_tensor(f"{n}{i}", [128, 512], F32) for i in range(2)]
        self.ps_proj, self.ps_m, self.ps_o, self.ps_sd = mk("ps_proj"), mk("ps_m"), mk("ps_o"), mk("ps_sd")
        self.b_proj = [Buf("pj0"), Buf("pj1")]
        self.b_m = [Buf("m0"), Buf("m1")]
        self.b_o = [Buf("o0"), Buf("o1")]
        self.b_sd = [Buf("sd0"), Buf("sd1")]
        self.mcount = 0
        self.pcount = 0

    def mbank(self):
        i = self.mcount % 2
        self.mcount += 1
        return i

    def pbank(self):
        i = self.pcount % 2
        self.pcount += 1
        return i

    def sb(self, name, shape, dt, persist=False):
        self.nalloc = getattr(self, "nalloc", 0) + 1
        name = f"{name}_{self.nalloc}"
        if persist:
            return self.nc.alloc_sbuf_tensor(name, list(shape), dt)
        return self.stack.enter_context(self.nc.sbuf_tensor(name, list(shape), dt))

    def new_phase(self):
        self.s.barrier()
        self.stack.close()
        self.stack = ExitStack()

    def din(self, name, shape, dt=F32):
        return self.nc.dram_tensor(name, list(shape), dt, kind="ExternalInput").ap()

    def dout(self, name, shape, dt=F32):
        return self.nc.dram_tensor(name, list(shape), dt, kind="ExternalOutput").ap()

    def dscr(self, name, shape, dt=F32):
        return self.nc.dram_tensor(name, list(shape), dt, kind="Internal").ap()


class WStream:
    def __init__(self, P, tiles, bufs, bbufs):
        self.P, self.tiles, self.bufs, self.bb = P, tiles, bufs, bbufs
        self.nw = len(bufs) if bufs else 0
        self.issued = 0

    def get(self, k):
        while self.issued < min(len(self.tiles), k + self.nw):
            i = self.issued % self.nw
            self.P.s.dma("pool", self.bufs[i][:, :, :], self.tiles[self.issued], writes=[self.bb[i]])
            self.issued += 1
        return k % self.nw


class Exchanger:
    def __init__(self, P, name, F, groups, xmode, nslots=4):
        self.P, self.F, self.groups, self.xmode, self.n = P, F, groups, xmode, nslots
        self.xin = [P.dscr(f"{name}_xin{k}", [128, F]) for k in range(nslots)]
        self.xout = [P.dscr(f"{name}_xout{k}", [256, F]) for k in range(nslots)]
        self.b_in = [Buf(f"xin{k}") for k in range(nslots)]
        self.b_out = [Buf(f"xout{k}") for k in range(nslots)]
        self.cnt = 0

    def send(self, S_f, b_Sf, Sp_f, b_Spf):
        s = self.P.s
        k = self.cnt % self.n
        self.cnt += 1
        s.dma("sp", self.xin[k], S_f, reads=(b_Sf if isinstance(b_Sf, list) else [b_Sf]), writes=[self.b_in[k]])
        if self.xmode == "cc":
            s.op("pool", lambda e: e.collective_compute("AllGather", ALU.bypass, replica_groups=self.groups,
                                                        ins=[self.xin[k]], outs=[self.xout[k]]),
                 [self.b_in[k]], [self.b_out[k]])
            s.dma("sp", Sp_f, self.xout[k][0:128, :], reads=[self.b_out[k]], writes=[b_Spf])
        else:
            s.dma("sp", Sp_f, self.xin[k], reads=[self.b_in[k]], writes=[b_Spf])

    def recv(self, Sp_f, b_Spf, Sp_b, b_Spb, flag, b_flag):
        self.P.s.op("dve", lambda e: e.tensor_scalar(Sp_b, Sp_f, flag, None, ALU.mult), [b_Spf, b_flag], [b_Spb])


def emit_layer0(P, C, xT, w_in, vecs, w_out, lnv, yT_d, rT_d, hT_out, groups, xmode, H=64, do_out=True):
    nc, s = P.nc, P.s
    ps_proj, b_proj, ps_m, b_m, ps_o, b_o, ps_sd, b_sd = P.ps_proj, P.b_proj, P.ps_m, P.b_m, P.ps_o, P.b_o, P.ps_sd, P.b_sd
    cst_f, ident, ones, eps_c, b_cst, flag, b_flag = C["cst_f"], C["ident"], C["ones128"], C["eps_c"], C["b_cst"], C["flag"], C["b_flag"]
    cmask = cst_f[0:CH, 256:256 + CH]
    smask = cst_f[:, 256 + CH:256 + CH + NTOK]
    mbank = P.mbank

    vec = P.sb("vec", [128, 5, 64], F32)
    lb = P.sb("lb", [128, 64], F32)
    oml = P.sb("oml", [128, 64], F32)
    noml = P.sb("noml", [128, 64], F32)
    lbt = P.sb("lbt", [128, 3, 64], F32)
    b_vec = Buf("vec")
    s.dma("sp", vec[:, :, :], vecs, writes=[b_vec])
    s.op("act", lambda e: e.activation(lbt[:, :, :], vec[:, 2:5, :], AF.Exp), [b_vec], [b_vec])
    s.op("dve", lambda e: e.tensor_add(lb[:, :], lbt[:, 0, :], lbt[:, 1, :]), [b_vec], [b_vec])
    s.op("dve", lambda e: e.tensor_add(lb[:, :], lb[:, :], lbt[:, 2, :]), [b_vec], [b_vec])
    s.op("dve", lambda e: e.reciprocal(lb[:, :], lb[:, :]), [b_vec], [b_vec])
    s.op("dve", lambda e: e.tensor_mul(lb[:, :], lb[:, :], lbt[:, 0, :]), [b_vec], [b_vec])
    s.op("dve", lambda e: e.tensor_scalar(oml[:, :], lb[:, :], -1.0, 1.0, ALU.mult, ALU.add), [b_vec], [b_vec])
    s.op("dve", lambda e: e.tensor_scalar(noml[:, :], lb[:, :], 1.0, -1.0, ALU.mult, ALU.add), [b_vec], [b_vec])

    hT = P.sb("hT", [128, KC, NTOK], BF16)
    b_hTg = [Buf(f"hT{g}") for g in range(8)]
    WS = WStream(P, [w_in[h, p] for h in range(H) for p in range(4)], None, None)
    for g in range(8):
        s.dma("pool", hT[:, g * 4:(g + 1) * 4, :], xT[g * 4:(g + 1) * 4].rearrange("k p t -> p k t"), writes=[b_hTg[g]])

    NW = 4
    wt = [P.sb(f"wt{i}", [128, KC, 128], BF16) for i in range(NW)]
    b_wt = [Buf(f"wt{i}") for i in range(NW)]
    q_f = P.sb("q_f", [128, NTOK], F32); b_qf = Buf("q_f")
    sg = P.sb("sg", [128, NTOK], F32); b_sg = Buf("sg")
    k_f = P.sb("k_f", [128, NTOK], F32); b_kf = Buf("k_f")
    lg = P.sb("lg", [128, NTOK], F32); b_lg = Buf("lg")
    bb = P.sb("bb", [128, NTOK], F32); b_bb = Buf("bb")
    e2 = P.sb("e2", [128, NTOK], F32); b_e2 = Buf("e2")
    vT = P.sb("vT", [128, NTOK], BF16); b_vT = Buf("vT")
    kdT = P.sb("kdT", [128, NTOK], BF16); b_kdT = Buf("kdT")
    pt = P.sb("pt", [128, NCH + 1], F32); b_pt = Buf("pt")
    zer = P.sb("zer", [128, NCH], F32)
    s.op("dve", lambda e: e.memset(pt[:, :], 1.0), [], [b_pt])
    s.op("dve", lambda e: e.memset(zer[:, :], 0.0), [], [b_pt])
    e1 = [P.sb(f"e1_{i}", [128, NTOK], F32) for i in range(2)]; b_e1 = [Buf("e1_0"), Buf("e1_1")]
    qd = [P.sb(f"qd{i}", [128, NTOK], BF16) for i in range(2)]; b_qd = [Buf("qd0"), Buf("qd1")]
    qg = [P.sb(f"qg{i}", [128, NTOK], BF16) for i in range(3)]; b_qg = [Buf(f"qg{i}") for i in range(3)]
    ki = [P.sb(f"ki{i}", [128, NTOK], BF16) for i in range(2)]; b_ki = [Buf("ki0"), Buf("ki1")]
    kdt = [P.sb(f"kdt{i}", [128, NCH, 128], BF16) for i in range(2)]; b_kdt = [Buf("kdt0"), Buf("kdt1")]
    vt = [P.sb(f"vt{i}", [128, NCH, 128], BF16) for i in range(2)]; b_vt = [Buf("vt0"), Buf("vt1")]
    a_sb = [P.sb(f"a_sb{i}", [128, NCH, CH], BF16) for i in range(2)]; b_asb = [Buf("a0"), Buf("a1")]
    sz = [P.sb(f"sz{i}", [128, NTOK], F32) for i in range(3)]; b_sz = [Buf("sz0"), Buf("sz1"), Buf("sz2")]
    S_f = P.sb("S_f", [128, 128], F32); b_Sf = Buf("S_f")
    S_b = [P.sb(f"S_b{i}", [128, 128], BF16) for i in range(2)]; b_Sb = [Buf("S_b0"), Buf("S_b1")]
    Sp_f = [P.sb(f"Sp_f{i}", [128, 128], F32) for i in range(2)]; b_Spf = [Buf("spf0"), Buf("spf1")]
    Sp_b = [P.sb(f"Sp_b{i}", [128, 128], BF16) for i in range(2)]; b_Spb = [Buf("spb0"), Buf("spb1")]
    o_f = [P.sb(f"o_f{i}", [128, NTOK], F32) for i in range(2)]; b_of = [Buf("of0"), Buf("of1")]
    sq = [P.sb(f"sq{i}", [128, TT], BF16) for i in range(3)]; b_sq = [Buf("sq0"), Buf("sq1"), Buf("sq2")]
    rstd = [P.sb(f"rstd{i}", [128, TT], F32) for i in range(2)]; b_rstd = [Buf("rs0"), Buf("rs1")]
    t1 = [P.sb(f"t1_{i}", [128, TT], F32) for i in range(2)]; b_t1 = [Buf("t10"), Buf("t11")]
    y_b = [P.sb(f"y_b{i}", [128, NTOK], BF16) for i in range(2)]; b_yb = [Buf("yb0"), Buf("yb1")]
    b_yTd = Buf("yT_d")
    X = Exchanger(P, "x0", 128, groups, xmode)
    wcount = [0]
    tsl = lambda tt: slice(tt * TT, (tt + 1) * TT)

    WS.bufs, WS.bb, WS.nw = wt, b_wt, NW

    def proj(wi, col, evac):
        for tt in range(NTT):
            pb = P.pbank()
            for kc in range(KC):
                s.op("pe", lambda e: e.matmul(ps_proj[pb][:, 0:TT], wt[wi][:, kc, :],
                                              hT[:, kc, tsl(tt)], start=(kc == 0), stop=(kc == KC - 1)),
                     [b_wt[wi], b_hTg[kc // 4]], [b_proj[pb]])
                if kc % 8 == 7:
                    yield
            evac(tt, ps_proj[pb][:, 0:TT], b_proj[pb])

    def stage1(h):
        par = h % 2
        hc = slice(h, h + 1)
        def ev_q(tt, ps, bps):
            s.op("act", lambda e: e.activation(q_f[:, tsl(tt)], ps, AF.Silu), [bps], [b_qf])
        yield from proj(WS.get(4 * h), 0, ev_q)

        def ev_f(tt, ps, bps):
            s.op("act", lambda e: e.activation(sg[:, tsl(tt)], ps, AF.Sigmoid, bias=vec[:, 0, hc], scale=1.0),
                 [bps, b_vec], [b_sg])
        yield from proj(WS.get(4 * h + 1), 128, ev_f)
        def ev_i(tt, ps, bps):
            s.op("act", lambda e: e.copy(vT[:, tsl(tt)], ps), [bps], [b_vT])
        gi = proj(WS.get(4 * h + 2), 0, ev_i)

        def chain():
            s.op("dve", lambda e: e.tensor_scalar(k_f[:, :], sg[:, :], noml[:, hc], oml[:, hc], ALU.mult, ALU.add),
                 [b_sg, b_vec], [b_kf])
            s.op("act", lambda e: e.activation(lg[:, :], sg[:, :], AF.Ln, bias=lb[:, hc], scale=oml[:, hc]),
                 [b_sg, b_vec], [b_lg])
            yield
            s.op("dve", lambda e: e.tensor_tensor_scan(bb[:, :], smask, lg[:, :], 0.0, ALU.mult, ALU.add),
                 [b_lg, b_cst], [b_bb])
            yield
            s.op("act", lambda e: e.activation(e1[par][:, :], bb[:, :], AF.Exp), [b_bb], [b_e1[par]])
            s.op("act", lambda e: e.activation(e2[:, :], bb[:, :], AF.Exp, scale=-1.0), [b_bb], [b_e2])
            yield
            e1c = e1[par][:, :].rearrange("p (c j) -> p c j", j=CH)
            s.op("dve", lambda e: e.tensor_copy(lg[:, 0:NCH], e1c[:, :, CH - 1]), [b_e1[par], b_lg], [b_lg])
            s.op("dve", lambda e: e.tensor_tensor_scan(pt[:, 1:NCH + 1], lg[:, 0:NCH], zer[:, :], 1.0, ALU.mult, ALU.add),
                 [b_lg, b_pt], [b_pt])
            s.op("dve", lambda e: e.tensor_tensor(q_f[:, :], q_f[:, :], e1[par][:, :], ALU.mult), [b_qf, b_e1[par]], [b_qf])
            yield
            s.op("act", lambda e: e.copy(qd[par][:, :], q_f[:, :]), [b_qf], [b_qd[par]])
            g3 = h % 3
            s.op("dve", lambda e: e.tensor_tensor(qg[g3][:, :].rearrange("p (c j) -> p c j", j=CH),
                                                  q_f[:, :].rearrange("p (c j) -> p c j", j=CH),
                                                  pt[:, 0:NCH].unsqueeze(2).to_broadcast([128, NCH, CH]), ALU.mult),
                 [b_qf, b_pt], [b_qg[g3]])
            yield
            s.op("dve", lambda e: e.tensor_tensor(e2[:, :], k_f[:, :], e2[:, :], ALU.mult), [b_kf, b_e2], [b_e2])
            yield
            s.op("act", lambda e: e.copy(ki[par][:, :], e2[:, :]), [b_e2], [b_ki[par]])
            e1l = e1c[:, :, CH - 1:CH].to_broadcast([128, NCH, CH])
            s.op("dve", lambda e: e.tensor_tensor(kdT[:, :].rearrange("p (c j) -> p c j", j=CH),
                                                  e2[:, :].rearrange("p (c j) -> p c j", j=CH), e1l, ALU.mult),
                 [b_e2, b_e1[par]], [b_kdT])
            yield


        for _ in chain():
            yield
            for _k in range(2):
                if next(gi, "done") != "done":
                    yield
        for _ in gi:
            yield

        def ev_z(tt, ps, bps):
            s.op("act", lambda e: e.activation(sz[h % 3][:, tsl(tt)], ps, AF.Silu), [bps], [b_sz[h % 3]])
        yield from proj(WS.get(4 * h + 3), 128, ev_z)
        for g in range(3):
            mb = mbank()
            for j in range(4):
                c = g * 4 + j
                cs = slice(c * CH, (c + 1) * CH)
                s.op("pe", lambda e: e.matmul(ps_m[mb][0:CH, j * 128:j * 128 + CH], ki[par][:, cs], qd[par][:, cs],
                                              start=True, stop=True), [b_ki[par], b_qd[par]], [b_m[mb]])
            s.op("dve", lambda e: e.tensor_tensor(a_sb[par][0:CH, g * 4:(g + 1) * 4, :],
                                                  ps_m[mb][0:CH, :].rearrange("p (c v) -> p c v", v=128)[:, :, 0:CH],
                                                  cmask.unsqueeze(1).to_broadcast([CH, 4, CH]), ALU.mult),
                 [b_m[mb], b_cst], [b_asb[par]])
        yield
        for g in range(3):
            mb = mbank()
            pv = ps_m[mb][:, :].bitcast(BF16)
            for j in range(4):
                c = g * 4 + j
                s.op("pe", lambda e: e.transpose(pv[0:CH, j * 128:(j + 1) * 128], kdT[:, c * CH:(c + 1) * CH], ident[:, :]),
                     [b_kdT, b_cst], [b_m[mb]])
            s.op("act", lambda e: e.copy(kdt[par][0:CH, g * 4:(g + 1) * 4, :],
                                         pv[0:CH, 0:512].rearrange("p (c v) -> p c v", v=128)),
                 [b_m[mb]], [b_kdt[par]])
        yield
        for g in range(3):
            mb = mbank()
            pv = ps_m[mb][:, :].bitcast(BF16)
            for j in range(4):
                c = g * 4 + j
                s.op("pe", lambda e: e.transpose(pv[0:CH, j * 128:(j + 1) * 128], vT[:, c * CH:(c + 1) * CH], ident[:, :]),
                     [b_vT, b_cst], [b_m[mb]])
            s.op("dve", lambda e: e.tensor_copy(vt[par][0:CH, g * 4:(g + 1) * 4, :],
                                                pv[0:CH, 0:512].rearrange("p (c v) -> p c v", v=128)),
                 [b_m[mb]], [b_vt[par]])
        yield

    def finalize(h):
        par = h % 2
        hc = slice(h, h + 1)
        g3 = h % 3
        X.recv(Sp_f[par][:, :], b_Spf[par], Sp_b[par][:, :], b_Spb[par], flag[:, 0:1], b_flag)
        for tt in range(NTT):
            mb = mbank()
            s.op("pe", lambda e: e.matmul(ps_m[mb][:, 0:TT], Sp_b[par][:, :], qg[g3][:, tsl(tt)], start=True, stop=True),
                 [b_Spb[par], b_qg[g3]], [b_m[mb]])
            s.op("dve", lambda e: e.tensor_tensor(o_f[par][:, tsl(tt)], o_f[par][:, tsl(tt)], ps_m[mb][:, 0:TT], ALU.add),
                 [b_of[par], b_m[mb]], [b_of[par]])
            s.op("act", lambda e: e.activation(sq[tt][:, :], o_f[par][:, tsl(tt)], AF.Square), [b_of[par]], [b_sq[tt]])
            yield
        for tt in range(NTT):
            ob = tt % 2
            mb = mbank()
            s.op("pe", lambda e: e.matmul(ps_m[mb][:, 0:TT], ones[:, :], sq[tt][:, :], start=True, stop=True),
                 [b_sq[tt], b_cst], [b_m[mb]])
            s.op("act", lambda e: e.activation(rstd[ob][:, :], ps_m[mb][:, 0:TT], AF.Ln, bias=eps_c[:, 0:1], scale=1.0),
                 [b_m[mb], b_cst], [b_rstd[ob]])
            s.op("act", lambda e: e.activation(rstd[ob][:, :], rstd[ob][:, :], AF.Exp, scale=-0.5),
                 [b_rstd[ob]], [b_rstd[ob]])
            s.op("dve", lambda e: e.scalar_tensor_tensor(t1[ob][:, :], o_f[par][:, tsl(tt)], vec[:, 1, hc], rstd[ob][:, :], ALU.mult, ALU.mult),
                 [b_of[par], b_rstd[ob], b_vec], [b_t1[ob]])
            s.op("dve", lambda e: e.tensor_tensor(y_b[par][:, tsl(tt)], t1[ob][:, :], sz[h % 3][:, tsl(tt)], ALU.mult),
                 [b_t1[ob], b_sz[h % 3]], [b_yb[par]])
            yield
        s.dma("sp", yT_d[h], y_b[par][:, :], reads=[b_yb[par]], writes=[b_yTd])

    def stage2(h):
        par = h % 2
        s.op("dve", lambda e: e.memset(S_f[:, :], 0.0), [], [b_Sf])
        s.op("dve", lambda e: e.memset(S_b[0][:, :], 0.0), [], [b_Sb[0]])
        for tt in range(NTT):
            ob = tt % 2
            for j in range(CPT):
                c = tt * CPT + j
                s.op("pe", lambda e: e.matmul(ps_sd[ob][:, j * 128:(j + 1) * 128], kdt[par][0:CH, c, :], vt[par][0:CH, c, :],
                                              start=True, stop=True), [b_kdt[par], b_vt[par]], [b_sd[ob]])
            for j in range(CPT):
                c = tt * CPT + j
                cs = slice(c * CH, (c + 1) * CH)
                osl = ps_o[ob][:, j * CH:(j + 1) * CH]
                sbi = c % 2
                s.op("pe", lambda e: e.matmul(osl, vt[par][0:CH, c, :], a_sb[par][0:CH, c, :], start=True, stop=False),
                     [b_vt[par], b_asb[par]], [b_o[ob]])
                s.op("pe", lambda e: e.matmul(osl, S_b[sbi][:, :], qd[par][:, cs], start=False, stop=True),
                     [b_Sb[sbi], b_qd[par]], [b_o[ob]])
                gl = e1[par][:, c * CH + CH - 1:c * CH + CH]
                sd = ps_sd[ob][:, j * 128:(j + 1) * 128]
                if c < NCH - 1:
                    s.op("dve", lambda e: e.scalar_tensor_tensor(S_b[1 - sbi][:, :], S_f[:, :], gl, sd, ALU.mult, ALU.add),
                         [b_Sf, b_sd[ob], b_e1[par]], [b_Sb[1 - sbi]])
                s.op("dve", lambda e: e.scalar_tensor_tensor(S_f[:, :], S_f[:, :], gl, sd, ALU.mult, ALU.add),
                     [b_Sf, b_sd[ob], b_e1[par]], [b_Sf])
                yield
            s.op("act", lambda e: e.copy(o_f[par][:, tsl(tt)], ps_o[ob][:, 0:TT]), [b_o[ob]], [b_of[par]])
            if tt == 1 and h > 0:
                yield from finalize(h - 1)
            yield
        X.send(S_f[:, :], b_Sf, Sp_f[par][:, :], b_Spf[par])

    for _ in stage1(0):
        pass
    for h in range(H):
        if h + 1 < H:
            interleave(stage1(h + 1), stage2(h), 2.7)
        else:
            for _ in stage2(h):
                pass
    for _ in finalize(H - 1):
        pass
    if not do_out:
        return
    P.new_phase()
    out_ln_phase(P, C, yT_d, w_out, lnv, xT, rT_d, hT_out, b_yTd)


def emit_layer1(P, C, xT, b_xT, w_in, wg1, wg2, vecs, w_out, lnv, yu_d, rs_d, rT_d, hT_out, groups, xmode, H=8, do_out=True):
    nc, s = P.nc, P.s
    ps_proj, b_proj, ps_m, b_m, ps_o, b_o, ps_sd, b_sd = P.ps_proj, P.b_proj, P.ps_m, P.b_m, P.ps_o, P.b_o, P.ps_sd, P.b_sd
    cst_f, ident, ones, eps_c, b_cst, flag, b_flag = C["cst_f"], C["ident"], C["ones1024"], C["eps_c"], C["b_cst"], C["flag"], C["b_flag"]
    cmask = cst_f[0:CH, 256:256 + CH]
    smask = cst_f[:, 256 + CH:256 + CH + NTOK]
    mbank = P.mbank
    ND, NV = 4, 8
    QSCALE = 512.0 ** -0.5

    vec = P.sb("vec1", [128, 96], F32)
    nbg = P.sb("nbg", [128, 32], F32)
    b_vec = Buf("vec1")
    s.dma("sp", vec[:, :], vecs, writes=[b_vec])
    s.op("dve", lambda e: e.tensor_scalar(nbg[:, :], vec[:, 64:96], -1.0, None, ALU.mult), [b_vec], [b_vec])

    hT = P.sb("hT1", [128, KC, NTOK], BF16)
    b_hTg = [Buf(f"hT1_{g}") for g in range(8)]
    for g in range(8):
        s.dma("pool", hT[:, g * 4:(g + 1) * 4, :], xT[g * 4:(g + 1) * 4].rearrange("k p t -> p k t"), reads=[b_xT], writes=[b_hTg[g]])
    wg1b = P.sb("wg1b", [128, KC, 16], BF16)
    wg2h = [P.sb(f"wg2h{i}", [16, 512], BF16) for i in range(2)]
    b_wg2 = [Buf("wg2h0"), Buf("wg2h1")]
    u_b = P.sb("u_b", [16, NTOK], BF16)
    b_g = Buf("gatew")
    s.dma("pool", wg1b[:, :, :], wg1, writes=[b_g])

    NW = 3
    wt = [P.sb(f"wu{i}", [128, KC, 128], BF16) for i in range(NW)]
    b_wt = [Buf(f"wu{i}") for i in range(NW)]
    sp_t = P.sb("sp_t", [128, NTOK], F32); b_sp = Buf("sp")
    bs_t = P.sb("bs_t", [128, NTOK], F32); b_bs = Buf("bs")
    e1 = P.sb("e1", [128, NTOK], F32); b_e1 = Buf("e1")
    e2 = P.sb("e2g", [128, NTOK], F32); b_e2 = Buf("e2")
    qd = P.sb("qd", [128, ND, NTOK], BF16); b_qd = Buf("qd")
    qg = P.sb("qg", [128, ND, NTOK], BF16); b_qg = Buf("qg")
    ki = P.sb("ki", [128, ND, NTOK], BF16); b_ki = Buf("ki")
    kdT = P.sb("kdT1", [128, NTOK], BF16); b_kdT = Buf("kdT")
    kdt = P.sb("kdt", [128, NCH, ND * 128], BF16); b_kdt = Buf("kdt")
    a_sb = P.sb("a_sb", [128, NCH, CH], BF16); b_asb = Buf("a_sb")
    glast = P.sb("glast", [128, ND, NCH], F32); b_gl = Buf("glast")
    pt = P.sb("pt1", [128, NCH + 1], F32); b_pt = Buf("pt1")
    zer = P.sb("zer1", [128, NCH], F32)
    s.op("dve", lambda e: e.memset(pt[:, :], 1.0), [], [b_pt])
    s.op("dve", lambda e: e.memset(zer[:, :], 0.0), [], [b_pt])
    vT = P.sb("vT1", [128, NTOK], BF16); b_vT = Buf("vT")
    vt_ = [P.sb(f"vtok{i}", [128, NCH, 128], BF16) for i in range(2)]; b_vt = [Buf("vt0"), Buf("vt1")]
    sz = [P.sb(f"sz1_{i}", [128, NTOK], F32) for i in range(3)]; b_sz = [Buf("sz0"), Buf("sz1"), Buf("sz2")]
    S_f = P.sb("S_f1", [128, ND * 128], F32); b_Sf = [Buf(f"S_f{d}") for d in range(ND)]
    S_b = [P.sb(f"S_b1_{i}", [128, ND * 128], BF16) for i in range(2)]
    b_Sb = [[Buf(f"S_b{i}_{d}") for d in range(ND)] for i in range(2)]
    Sp_f = [P.sb(f"Sp_f1_{i}", [128, ND * 128], F32) for i in range(2)]; b_Spf = [Buf("spf0"), Buf("spf1")]
    Sp_b = [P.sb(f"Sp_b1_{i}", [128, ND * 128], BF16) for i in range(2)]; b_Spb = [Buf("spb0"), Buf("spb1")]
    o_f = [P.sb(f"o_f1_{i}", [128, NTOK], F32) for i in range(2)]; b_of = [Buf("of0"), Buf("of1")]
    sq = [P.sb(f"sq1_{i}", [128, TT], BF16) for i in range(3)]; b_sq = [Buf("sq0"), Buf("sq1"), Buf("sq2")]
    yu_t = [P.sb(f"yu{i}", [128, TT], F32) for i in range(2)]; b_yu = [Buf("yu0"), Buf("yu1")]
    ssacc = P.sb("ssacc", [128, NTOK], F32); b_ss = Buf("ssacc")
    b_yud = Buf("yu_d")
    X = Exchanger(P, "x1", ND * 128, groups, xmode)
    tsl = lambda tt: slice(tt * TT, (tt + 1) * TT)
    dsl = lambda dt: slice(dt * 128, (dt + 1) * 128)

    for tt in range(NTT):
        pb = P.pbank()
        for kc in range(KC):
            s.op("pe", lambda e: e.matmul(ps_proj[pb][0:16, 0:TT], wg1b[:, kc, :], hT[:, kc, tsl(tt)],
                                          start=(kc == 0), stop=(kc == KC - 1)), [b_g, b_hTg[kc // 4]], [b_proj[pb]])
        s.op("act", lambda e: e.copy(u_b[0:16, tsl(tt)], ps_proj[pb][0:16, 0:TT]), [b_proj[pb]], [b_g])

    WS = WStream(P, [w_in[hd, ti] for hd in range(H) for ti in range(24)], wt, b_wt)

    def load_w(hd, ti):
        return WS.get(hd * 24 + ti)

    def proj(wi, evac):
        for tt in range(NTT):
            pb = P.pbank()
            for kc in range(KC):
                s.op("pe", lambda e: e.matmul(ps_proj[pb][:, 0:TT], wt[wi][:, kc, :], hT[:, kc, tsl(tt)],
                                              start=(kc == 0), stop=(kc == KC - 1)), [b_wt[wi], b_hTg[kc // 4]], [b_proj[pb]])
                if kc % 8 == 7:
                    yield
            evac(tt, ps_proj[pb][:, 0:TT], b_proj[pb])

    def stageA(hd, inject=None):
        hp = hd % 2
        s.dma("pool", wg2h[hp][:, :], wg2[:, hd * 512:(hd + 1) * 512], writes=[b_wg2[hp]])
        for dt in range(ND):
            gdt = hd * ND + dt
            wq = load_w(hd, 2 * dt)
            for tt in range(NTT):
                pb = P.pbank()
                s.op("pe", lambda e: e.matmul(ps_proj[pb][:, 0:TT], wg2h[hp][0:16, dsl(dt)], u_b[0:16, tsl(tt)],
                                              start=True, stop=True), [b_g, b_wg2[hp]], [b_proj[pb]])
                s.op("act", lambda e: e.activation(sp_t[:, tsl(tt)], ps_proj[pb][:, 0:TT], AF.Exp, bias=nbg[:, gdt:gdt + 1], scale=-1.0),
                     [b_proj[pb], b_vec], [b_sp])
            s.op("act", lambda e: e.activation(sp_t[:, :], sp_t[:, :], AF.Ln, bias=1.0, scale=1.0), [b_sp], [b_sp])
            s.op("dve", lambda e: e.tensor_tensor_scan(bs_t[:, :], smask, sp_t[:, :], 0.0, ALU.mult, ALU.add),
                 [b_sp, b_cst], [b_bs])
            s.op("act", lambda e: e.activation(e1[:, :], bs_t[:, :], AF.Exp, scale=-1.0 / 16), [b_bs], [b_e1])
            s.op("act", lambda e: e.activation(e2[:, :], bs_t[:, :], AF.Exp, scale=1.0 / 16), [b_bs], [b_e2])
            e1c = e1[:, :].rearrange("p (c j) -> p c j", j=CH)
            s.op("dve", lambda e: e.tensor_copy(glast[:, dt, :], e1c[:, :, CH - 1]), [b_e1], [b_gl])
            s.op("dve", lambda e: e.tensor_tensor_scan(pt[:, 1:NCH + 1], glast[:, dt, :], zer[:, :], 1.0, ALU.mult, ALU.add),
                 [b_gl, b_pt], [b_pt])
            yield

            def ev_q(tt, ps, bps):
                s.op("dve", lambda e: e.scalar_tensor_tensor(bs_t[:, tsl(tt)], ps, QSCALE, e1[:, tsl(tt)], ALU.mult, ALU.mult),
                     [bps, b_e1], [b_bs])
            yield from proj(wq, ev_q)
            if dt == 0 and inject is not None:
                inject()
            s.op("act", lambda e: e.copy(qd[:, dt, :], bs_t[:, :]), [b_bs], [b_qd])
            s.op("dve", lambda e: e.tensor_tensor(qg[:, dt, :].rearrange("p (c j) -> p c j", j=CH),
                                                  bs_t[:, :].rearrange("p (c j) -> p c j", j=CH),
                                                  pt[:, 0:NCH].unsqueeze(2).to_broadcast([128, NCH, CH]), ALU.mult),
                 [b_bs, b_pt], [b_qg])

            def ev_k(tt, ps, bps):
                s.op("dve", lambda e: e.tensor_tensor(sp_t[:, tsl(tt)], ps, e2[:, tsl(tt)], ALU.mult), [bps, b_e2], [b_sp])
            wk = load_w(hd, 2 * dt + 1)
            yield from proj(wk, ev_k)
            s.op("act", lambda e: e.copy(ki[:, dt, :], sp_t[:, :]), [b_sp], [b_ki])
            e1l = e1c[:, :, CH - 1:CH].to_broadcast([128, NCH, CH])
            s.op("dve", lambda e: e.tensor_tensor(kdT[:, :].rearrange("p (c j) -> p c j", j=CH),
                                                  sp_t[:, :].rearrange("p (c j) -> p c j", j=CH), e1l, ALU.mult),
                 [b_sp, b_e1], [b_kdT])
            for g in range(3):
                mb = mbank()
                pv = ps_m[mb][:, :].bitcast(BF16)
                for j in range(4):
                    c = g * 4 + j
                    s.op("pe", lambda e: e.transpose(pv[0:CH, j * 128:(j + 1) * 128], kdT[:, c * CH:(c + 1) * CH], ident[:, :]),
                         [b_kdT, b_cst], [b_m[mb]])
                s.op("act", lambda e: e.copy(kdt[0:CH, g * 4:(g + 1) * 4, dsl(dt)],
                                             pv[0:CH, 0:512].rearrange("p (c v) -> p c v", v=128)),
                     [b_m[mb]], [b_kdt])
            yield
        for g in range(3):
            mb = mbank()
            for j in range(4):
                c = g * 4 + j
                cs = slice(c * CH, (c + 1) * CH)
                for dt in range(ND):
                    s.op("pe", lambda e: e.matmul(ps_m[mb][0:CH, j * 128:j * 128 + CH], ki[:, dt, cs], qd[:, dt, cs],
                                                  start=(dt == 0), stop=(dt == ND - 1)), [b_ki, b_qd], [b_m[mb]])
            s.op("dve", lambda e: e.tensor_tensor(a_sb[0:CH, g * 4:(g + 1) * 4, :],
                                                  ps_m[mb][0:CH, :].rearrange("p (c v) -> p c v", v=128)[:, :, 0:CH],
                                                  cmask.unsqueeze(1).to_broadcast([CH, 4, CH]), ALU.mult),
                 [b_m[mb], b_cst], [b_asb])
        yield

    def stageB(hd, v):
        par = v % 2
        wv = load_w(hd, 8 + 2 * v)

        def ev_v(tt, ps, bps):
            s.op("act", lambda e: e.copy(vT[:, tsl(tt)], ps), [bps], [b_vT])
        yield from proj(wv, ev_v)
        wz = load_w(hd, 8 + 2 * v + 1)

        def ev_z(tt, ps, bps):
            s.op("act", lambda e: e.activation(sz[v % 3][:, tsl(tt)], ps, AF.Silu), [bps], [b_sz[v % 3]])
        yield from proj(wz, ev_z)
        for g in range(3):
            mb = mbank()
            pv = ps_m[mb][:, :].bitcast(BF16)
            for j in range(4):
                c = g * 4 + j
                s.op("pe", lambda e: e.transpose(pv[0:CH, j * 128:(j + 1) * 128], vT[:, c * CH:(c + 1) * CH], ident[:, :]),
                     [b_vT, b_cst], [b_m[mb]])
            s.op("dve", lambda e: e.tensor_copy(vt_[par][0:CH, g * 4:(g + 1) * 4, :],
                                                pv[0:CH, 0:512].rearrange("p (c v) -> p c v", v=128)),
                 [b_m[mb]], [b_vt[par]])

    def finalize(hd, v):
        par = v % 2
        cc = hd * NV + v
        X.recv(Sp_f[par][:, :], b_Spf[par], Sp_b[par][:, :], b_Spb[par], flag[:, 0:1], b_flag)
        for tt in range(NTT):
            mb = mbank()
            for dt in range(ND):
                s.op("pe", lambda e: e.matmul(ps_m[mb][:, 0:TT], Sp_b[par][:, dsl(dt)], qg[:, dt, tsl(tt)],
                                              start=(dt == 0), stop=(dt == ND - 1)), [b_Spb[par], b_qg], [b_m[mb]])
            s.op("dve", lambda e: e.tensor_tensor(o_f[par][:, tsl(tt)], o_f[par][:, tsl(tt)], ps_m[mb][:, 0:TT], ALU.add),
                 [b_of[par], b_m[mb]], [b_of[par]])
            s.op("act", lambda e: e.activation(sq[tt][:, :], o_f[par][:, tsl(tt)], AF.Square), [b_of[par]], [b_sq[tt]])
            yield
        for tt in range(NTT):
            ob = tt % 2
            mb = mbank()
            s.op("pe", lambda e: e.matmul(ps_m[mb][:, 0:TT], ones[:, :], sq[tt][:, :], start=True, stop=True),
                 [b_sq[tt], b_cst], [b_m[mb]])
            if v == 0:
                s.op("dve", lambda e: e.tensor_copy(ssacc[:, tsl(tt)], ps_m[mb][:, 0:TT]), [b_m[mb]], [b_ss])
            else:
                s.op("dve", lambda e: e.tensor_tensor(ssacc[:, tsl(tt)], ssacc[:, tsl(tt)], ps_m[mb][:, 0:TT], ALU.add),
                     [b_m[mb], b_ss], [b_ss])
            s.op("dve", lambda e: e.scalar_tensor_tensor(yu_t[ob][:, :], o_f[par][:, tsl(tt)], vec[:, cc:cc + 1], sz[v % 3][:, tsl(tt)],
                                                         ALU.mult, ALU.mult), [b_of[par], b_sz[v % 3], b_vec], [b_yu[ob]])
            s.dma("sp", yu_d[cc][:, tsl(tt)], yu_t[ob][:, :], reads=[b_yu[ob]], writes=[b_yud])
            yield

    def stageC(hd, v):
        par = v % 2
        s.op("dve", lambda e: e.memset(S_f[:, :], 0.0), [], b_Sf)
        s.op("dve", lambda e: e.memset(S_b[0][:, :], 0.0), [], b_Sb[0])
        for tt in range(NTT):
            ob = tt % 2
            for j in range(CPT):
                c = tt * CPT + j
                cs = slice(c * CH, (c + 1) * CH)
                sb_ = c % 2
                sbi = c % 2
                for dt in range(ND):
                    s.op("pe", lambda e: e.matmul(ps_sd[sb_][:, dsl(dt)], kdt[0:CH, c, dsl(dt)],
                                                  vt_[par][0:CH, c, :], start=True, stop=True), [b_kdt, b_vt[par]], [b_sd[sb_]])
                osl = ps_o[ob][:, j * CH:(j + 1) * CH]
                s.op("pe", lambda e: e.matmul(osl, vt_[par][0:CH, c, :], a_sb[0:CH, c, :], start=True, stop=False),
                     [b_vt[par], b_asb], [b_o[ob]])
                for dt in range(ND):
                    s.op("pe", lambda e: e.matmul(osl, S_b[sbi][:, dsl(dt)], qd[:, dt, cs], start=False, stop=(dt == ND - 1)),
                         [b_Sb[sbi][dt], b_qd], [b_o[ob]])
                if c < NCH - 1:
                    for dt in range(ND):
                        s.op("dve", lambda e: e.scalar_tensor_tensor(S_b[1 - sbi][:, dsl(dt)], S_f[:, dsl(dt)], glast[:, dt, c:c + 1],
                                                                     ps_sd[sb_][:, dsl(dt)], ALU.mult, ALU.add),
                             [b_Sf[dt], b_sd[sb_], b_gl], [b_Sb[1 - sbi][dt]])
                for dt in range(ND):
                    s.op("dve", lambda e: e.scalar_tensor_tensor(S_f[:, dsl(dt)], S_f[:, dsl(dt)], glast[:, dt, c:c + 1],
                                                                 ps_sd[sb_][:, dsl(dt)], ALU.mult, ALU.add),
                         [b_Sf[dt], b_sd[sb_], b_gl], [b_Sf[dt]])
                yield
            s.op("act", lambda e: e.copy(o_f[par][:, tsl(tt)], ps_o[ob][:, 0:TT]), [b_o[ob]], [b_of[par]])
            if tt == 1 and v > 0:
                yield from finalize(hd, v - 1)
            yield
        X.send(S_f[:, :], b_Sf, Sp_f[par][:, :], b_Spf[par])

    def head_end(hd):
        s.op("act", lambda e: e.activation(ssacc[:, :], ssacc[:, :], AF.Ln, bias=eps_c[:, 0:1], scale=1.0), [b_ss, b_cst], [b_ss])
        s.op("act", lambda e: e.activation(ssacc[:, :], ssacc[:, :], AF.Exp, scale=-0.5), [b_ss], [b_ss])
        s.dma("sp", rs_d[hd], ssacc[:, :], reads=[b_ss], writes=[b_yud])

    def tail(hd):
        for _ in finalize(hd, NV - 1):
            pass
        head_end(hd)

    for hd in range(H):
        for _ in stageA(hd, inject=(lambda p=hd - 1: tail(p)) if hd > 0 else None):
            pass
        for _ in stageB(hd, 0):
            pass
        for v in range(NV):
            if v + 1 < NV:
                interleave(stageB(hd, v + 1), stageC(hd, v), 1.12)
            else:
                for _ in stageC(hd, v):
                    pass
    tail(H - 1)
    if not do_out:
        return
    P.new_phase()
    out_ln_phase(P, C, None, w_out, lnv, xT, rT_d, hT_out, None, yu=(yu_d, rs_d, b_yud))


def out_ln_phase(P, C, yT_d, w_out, lnv, resT, rT_d, hT_out, b_yTd, yu=None):
    nc, s = P.nc, P.s
    cst_f, b_cst = C["cst_f"], C["b_cst"]
    ps_acc, b_acc = P.ps_proj, P.b_proj
    ps_st = [P.ps_m[0], P.ps_m[1], P.ps_o[0], P.ps_o[1], P.ps_sd[0], P.ps_sd[1]]
    b_st = [P.b_m[0], P.b_m[1], P.b_o[0], P.b_o[1], P.b_sd[0], P.b_sd[1]]
    lnp = P.sb("lnp", [128, 2, KC], F32); b_lnp = Buf("lnp")
    mean = P.sb("mean", [128, NTOK], F32); b_mean = Buf("mean")
    rstd = P.sb("rstdl", [128, NTOK], F32); b_rstd = Buf("rstdl")
    lneps = P.sb("lneps", [128, 1], F32)
    onesd = P.sb("onesd", [128, 128], BF16)
    res = [P.sb(f"res{i}", [128, NTOK], F32) for i in range(2)]; b_res = [Buf(f"res{i}") for i in range(2)]
    rr = [P.sb(f"rr{i}", [128, NTOK], F32) for i in range(2)]; b_rr = [Buf(f"rr{i}") for i in range(2)]
    s.op("dve", lambda e: e.tensor_scalar(onesd[:, :], cst_f[:, 128:256], 128.0 / D, None, ALU.mult), [b_cst], [b_cst])
    s.op("dve", lambda e: e.memset(lneps[:, :], LN_EPS), [], [b_rstd])
    s.dma("sp", lnp[:, :, :], lnv, writes=[b_lnp])
    outer_stack = P.stack
    P.stack = ExitStack()
    yT = P.sb("yT", [128, 64, NTOK], BF16)
    b_yT = [Buf(f"yT{c}") for c in range(64)]
    if yu is None:
        for c in range(64):
            s.dma("sp", yT[:, c, :], yT_d[c], reads=[b_yTd], writes=[b_yT[c]])
    else:
        yu_d, rs_d, b_yud = yu
        stg = res + [P.sb(f"yst{i}", [128, NTOK], F32) for i in range(2)]
        b_stg = b_res + [Buf("yst0"), Buf("yst1")]
        for c in range(64):
            hd = c // 8
            i = c % 4
            if c % 8 == 0:
                s.dma("sp", rr[hd % 2][:, :], rs_d[hd], reads=[b_yud], writes=[b_rr[hd % 2]])
            s.dma("sp", stg[i][:, :], yu_d[c], reads=[b_yud], writes=[b_stg[i]])
            s.op("dve", lambda e: e.tensor_tensor(yT[:, c, :], stg[i][:, :], rr[hd % 2][:, :], ALU.mult),
                 [b_stg[i], b_rr[hd % 2]], [b_yT[c]])
    NWO = 2
    wo = [P.sb(f"wo{i}", [128, 64, 128], BF16) for i in range(NWO)]; b_wo = [Buf(f"wo{i}") for i in range(NWO)]
    rhi = [P.sb(f"rhi{i}", [128, TT], BF16) for i in range(2)]; b_rhi = [Buf(f"rhi{i}") for i in range(2)]
    rlo = [P.sb(f"rlo{i}", [128, TT], BF16) for i in range(2)]; b_rlo = [Buf(f"rlo{i}") for i in range(2)]
    rsq = [P.sb(f"rsq{i}", [128, TT], BF16) for i in range(2)]; b_rsq = [Buf(f"rsq{i}") for i in range(2)]
    b_rTd = [Buf(f"rTd{k}") for k in range(KC)]
    pc = 0
    WSO = WStream(P, [w_out[dm] for dm in range(KC)], wo, b_wo)
    pending = []

    def flush_stats():
        while pending:
            tt_, si_, first, last = pending.pop(0)
            s.op("pe", lambda e: e.matmul(ps_st[tt_][:, 0:TT], onesd[:, :], rhi[si_][:, :], start=first, stop=False),
                 [b_rhi[si_], b_cst], [b_st[tt_]])
            s.op("pe", lambda e: e.matmul(ps_st[tt_][:, 0:TT], onesd[:, :], rlo[si_][:, :], start=False, stop=last),
                 [b_rlo[si_], b_cst], [b_st[tt_]])
            s.op("pe", lambda e: e.matmul(ps_st[3 + tt_][:, 0:TT], onesd[:, :], rsq[si_][:, :], start=first, stop=last),
                 [b_rsq[si_], b_cst], [b_st[3 + tt_]])

    for dm in range(KC):
        wi = WSO.get(dm)
        ri = dm % 2
        s.dma("sp", res[ri][:, :], resT[dm], writes=[b_res[ri]])
        for tt in range(NTT):
            pb = P.pbank()
            pc += 1
            ts = slice(tt * TT, (tt + 1) * TT)
            for c in range(64):
                s.op("pe", lambda e: e.matmul(ps_acc[pb][:, 0:TT], wo[wi][:, c, :], yT[:, c, ts], start=(c == 0), stop=(c == 63)),
                     [b_wo[wi], b_yT[c]], [b_acc[pb]])
            flush_stats()
            si = pc % 2
            s.op("dve", lambda e: e.scalar_tensor_tensor(rr[ri][:, ts], res[ri][:, ts], ALPHA, ps_acc[pb][:, 0:TT], ALU.mult, ALU.add),
                 [b_res[ri], b_acc[pb]], [b_rr[ri]])
            s.op("act", lambda e: e.copy(rhi[si][:, :], rr[ri][:, ts]), [b_rr[ri]], [b_rhi[si]])
            s.op("dve", lambda e: e.tensor_tensor(rlo[si][:, :], rr[ri][:, ts], rhi[si][:, :], ALU.subtract),
                 [b_rr[ri], b_rhi[si]], [b_rlo[si]])
            s.op("act", lambda e: e.activation(rsq[si][:, :], rr[ri][:, ts], AF.Square), [b_rr[ri]], [b_rsq[si]])
            pending.append((tt, si, dm == 0, dm == KC - 1))
        s.dma("sp", rT_d[dm], rr[ri][:, :], reads=[b_rr[ri]], writes=[b_rTd[dm]])
    flush_stats()
    for tt in range(NTT):
        ts = slice(tt * TT, (tt + 1) * TT)
        s.op("act", lambda e: e.copy(mean[:, ts], ps_st[tt][:, 0:TT]), [b_st[tt]], [b_mean])
        s.op("dve", lambda e: e.tensor_tensor(rstd[:, ts], mean[:, ts], mean[:, ts], ALU.mult), [b_mean], [b_rstd])
        s.op("dve", lambda e: e.tensor_tensor(rstd[:, ts], ps_st[3 + tt][:, 0:TT], rstd[:, ts], ALU.subtract), [b_st[3 + tt], b_rstd], [b_rstd])
    s.op("act", lambda e: e.activation(rstd[:, :], rstd[:, :], AF.Ln, bias=lneps[:, 0:1], scale=1.0), [b_rstd], [b_rstd])
    s.op("act", lambda e: e.activation(rstd[:, :], rstd[:, :], AF.Exp, scale=-0.5), [b_rstd], [b_rstd])
    b_out = Buf("hT_out")
    s.barrier()
    P.stack.close()
    P.stack = outer_stack
    for i in range(2, 4):
        res.append(P.sb(f"res{i}", [128, NTOK], F32)); b_res.append(Buf(f"res{i}"))
        rr.append(P.sb(f"rr{i}", [128, NTOK], F32)); b_rr.append(Buf(f"rr{i}"))
    for kc in range(KC):
        ri = kc % 4
        s.dma("sp", rr[ri][:, :], rT_d[kc], reads=[b_rTd[kc]], writes=[b_rr[ri]])
        s.op("dve", lambda e: e.tensor_tensor(res[ri][:, :], rr[ri][:, :], mean[:, :], ALU.subtract), [b_rr[ri], b_mean], [b_res[ri]])
        s.op("dve", lambda e: e.tensor_tensor(res[ri][:, :], res[ri][:, :], rstd[:, :], ALU.mult), [b_res[ri], b_rstd], [b_res[ri]])
        s.op("act", lambda e: e.activation(res[ri][:, :], res[ri][:, :], AF.Identity, bias=lnp[:, 1, kc:kc + 1], scale=lnp[:, 0, kc:kc + 1]),
             [b_res[ri], b_lnp], [b_res[ri]])
        s.dma("pool", hT_out[kc], res[ri][:, :], reads=[b_res[ri]], writes=[b_out])
    s.drain_dma("sp")
    s.drain_dma("pool")
    return b_out


def build_fused(H0=64, H1=8, groups=None, xmode="cc", layers=(0, 1), do_out=True, debug=False):
    P = Prog()
    nc, s = P.nc, P.s
    groups = groups or [[0, 1], [2, 3], [4, 5], [6, 7]]
    cst = P.din("cst", [128, 128 + 128 + CH + NTOK])
    flag_d = P.din("flag", [128, 1])
    cst_f = P.sb("cst_f", [128, 128 + 128 + CH + NTOK], F32, persist=True)
    ident = P.sb("ident", [128, 128], BF16, persist=True)
    ones128 = P.sb("ones128", [128, 128], BF16, persist=True)
    ones1024 = P.sb("ones1024", [128, 128], BF16, persist=True)
    eps_c = P.sb("eps_c", [128, 1], F32, persist=True)
    flag = P.sb("flag_sb", [128, 1], F32, persist=True)
    b_cst, b_flag = Buf("cst"), Buf("flag")
    s.dma("sp", cst_f[:, :], cst, writes=[b_cst])
    s.dma("sp", flag[:, :], flag_d, writes=[b_flag])
    s.op("dve", lambda e: e.memset(eps_c[:, :], RMS_EPS), [], [b_cst])
    s.op("dve", lambda e: e.tensor_copy(ident[:, :], cst_f[:, 0:128]), [b_cst], [b_cst])
    s.op("dve", lambda e: e.tensor_copy(ones128[:, :], cst_f[:, 128:256]), [b_cst], [b_cst])
    s.op("dve", lambda e: e.tensor_scalar(ones1024[:, :], cst_f[:, 128:256], 128.0 / 1024, None, ALU.mult), [b_cst], [b_cst])
    C = dict(cst_f=cst_f, ident=ident, ones128=ones128, ones1024=ones1024, eps_c=eps_c, b_cst=b_cst, flag=flag, b_flag=b_flag)
    dbg = P.dout if debug else P.dscr
    xT = P.din("xT", [KC, 128, NTOK])
    b_h1 = Buf("h1T")
    if 0 in layers:
        w_in0 = P.din("w_in0", [H0, 4, 128, KC, 128])
        vecs0 = P.din("vecs0", [128, 5, 64])
        yT_d = dbg("yT_d", [64, 128, NTOK], BF16)
        if do_out:
            w_out0 = P.din("w_out0", [KC, 128, 64, 128])
            lnv0 = P.din("lnv0", [128, 2, KC])
            rT_d = P.dscr("rT_d", [KC, 128, NTOK], F32)
            h1T = (P.dout if (debug or 1 not in layers) else P.dscr)("h1T", [KC, 128, NTOK], F32)
        else:
            w_out0 = lnv0 = rT_d = h1T = None
        emit_layer0(P, C, xT, w_in0, vecs0, w_out0, lnv0, yT_d, rT_d, h1T, groups, xmode, H=H0, do_out=do_out)
        if do_out:
            P.new_phase()
        x1 = h1T
    else:
        x1 = xT
    if 1 in layers:
        w_in1 = P.din("w_in1", [H1, 24, 128, KC, 128])
        wg1 = P.din("wg1", [128, KC, 16])
        wg2 = P.din("wg2", [16, 4096])
        vecs1 = P.din("vecs1", [128, 96])
        yu_d = dbg("yu_d", [64, 128, NTOK], F32)
        rs_d = dbg("rs_d", [8, 128, NTOK], F32)
        if do_out:
            w_out1 = P.din("w_out1", [KC, 128, 64, 128])
            lnv1 = P.din("lnv1", [128, 2, KC])
            rT_d1 = P.dscr("rT_d1", [KC, 128, NTOK], F32)
            outT = P.dout("outT", [KC, 128, NTOK], F32)
        else:
            w_out1 = lnv1 = rT_d1 = outT = None
        emit_layer1(P, C, x1, b_h1, w_in1, wg1, wg2, vecs1, w_out1, lnv1, yu_d, rs_d, rT_d1, outT, groups, xmode, H=H1, do_out=do_out)
    s.barrier()
    return P


def _relayout_w_in0(w):
    return np.ascontiguousarray(w.reshape(32, 128, 4, 64, 128).transpose(3, 2, 1, 0, 4))


def _relayout_w_out(w):
    return np.ascontiguousarray(w.reshape(64, 128, 32, 128).transpose(2, 1, 0, 3))


def _relayout_w_in1(w):
    wr = w.reshape(32, 128, 24576)
    out = np.empty((8, 24, 128, 32, 128), np.float32)
    for hd in range(8):
        for dt in range(4):
            out[hd, 2 * dt] = wr[:, :, hd * 512 + dt * 128: hd * 512 + (dt + 1) * 128].transpose(1, 0, 2)
            out[hd, 2 * dt + 1] = wr[:, :, 4096 + hd * 512 + dt * 128: 4096 + hd * 512 + (dt + 1) * 128].transpose(1, 0, 2)
        for v in range(8):
            out[hd, 8 + 2 * v] = wr[:, :, 8192 + hd * 1024 + v * 128: 8192 + hd * 1024 + (v + 1) * 128].transpose(1, 0, 2)
            out[hd, 9 + 2 * v] = wr[:, :, 16384 + hd * 1024 + v * 128: 16384 + hd * 1024 + (v + 1) * 128].transpose(1, 0, 2)
    return out


def _make_cst():
    cst = np.zeros((128, 256 + CH + NTOK), np.float32)
    cst[:, 0:128] = np.eye(128, dtype=np.float32)
    cst[:, 128:256] = 1.0 / 128
    jj = np.arange(CH)
    cst[0:CH, 256:256 + CH] = (jj[:, None] <= jj[None, :]).astype(np.float32)
    sm = np.ones(NTOK, np.float32)
    sm[::CH] = 0
    cst[:, 256 + CH:] = sm[None, :]
    return cst


_PROG = {}


def kernel(x, meta, lb_logits, l0_w_in, l0_b_f, l0_norm_g, l0_w_out, l0_ln_g, l0_ln_b,
           l1_w_in, l1_w_g1, l1_w_g2, l1_b_g, l1_norm_g, l1_w_out, l1_ln_g, l1_ln_b):
    f = lambda a: np.ascontiguousarray(np.asarray(a, dtype=np.float32))
    x, meta = f(x), f(meta)
    B = x.shape[0]
    h0 = np.concatenate([np.broadcast_to(meta[None], (B, 16, D)), x], axis=1)
    lbl = f(lb_logits)
    vecs0 = np.stack([f(l0_b_f), f(l0_norm_g), lbl[0], lbl[1], lbl[2]], 0)
    shared = {
        "cst": _make_cst(),
        "w_in0": _relayout_w_in0(f(l0_w_in)),
        "vecs0": np.ascontiguousarray(vecs0.reshape(5, 64, 128).transpose(2, 0, 1)),
        "w_out0": _relayout_w_out(f(l0_w_out)),
        "lnv0": np.ascontiguousarray(np.stack([f(l0_ln_g), f(l0_ln_b)], 0).reshape(2, 32, 128).transpose(2, 0, 1)),
        "w_in1": _relayout_w_in1(f(l1_w_in)),
        "wg1": np.ascontiguousarray(f(l1_w_g1).reshape(32, 128, 16).transpose(1, 0, 2)),
        "wg2": f(l1_w_g2),
        "vecs1": np.ascontiguousarray(np.concatenate([f(l1_norm_g).reshape(64, 128).T, f(l1_b_g).reshape(32, 128).T], 1)),
        "w_out1": _relayout_w_out(f(l1_w_out)),
        "lnv1": np.ascontiguousarray(np.stack([f(l1_ln_g), f(l1_ln_b)], 0).reshape(2, 32, 128).transpose(2, 0, 1)),
    }
    in_maps = []
    for c in range(NCORES):
        b, half = c // 2, c % 2
        d = dict(shared)
        d["xT"] = np.ascontiguousarray(h0[b, half * NTOK:(half + 1) * NTOK].T).reshape(KC, 128, NTOK)
        d["flag"] = np.full((128, 1), float(half), np.float32)
        in_maps.append(d)
    if "p" not in _PROG:
        _PROG["p"] = build_fused()
    res = run_bass_kernel_spmd(_PROG["p"].nc, in_maps, core_ids=list(range(NCORES)))
    out = np.empty((B, 2064, D), np.float32)
    for c in range(NCORES):
        b, half = c // 2, c % 2
        out[b, half * NTOK:(half + 1) * NTOK] = res.results[c]["outT"].reshape(D, NTOK).T
    return np.ascontiguousarray(out[:, 16:])
```

```python
import numpy as np
import concourse.bass as bass
import concourse.mybir as mybir
from concourse.bass_utils import run_bass_kernel_spmd
from contextlib import ExitStack

F32 = mybir.dt.float32
BF16 = mybir.dt.bfloat16
AF = mybir.ActivationFunctionType
ALU = mybir.AluOpType

D = 4096
DI = 8192
NTOK = 1032
CH = 86
NCH = 12
TT = 344
NTT = 3
CPT = 4
KC = 32
NCORES = 8
ALPHA = (2.0 * 2) ** 0.25
LN_EPS = 1e-5
RMS_EPS = 1e-6


class Buf:
    __slots__ = ("name", "w", "r")

    def __init__(self, name=""):
        self.name = name
        self.w = None
        self.r = []


class Sched:
    ENG = ("pe", "act", "dve", "pool", "sp")

    def __init__(self, nc, n_dma_sems=16):
        self.nc = nc
        self.eng = {"pe": nc.tensor, "act": nc.scalar, "dve": nc.vector,
                    "pool": nc.gpsimd, "sp": nc.sync}
        self.sems = {}
        self.cnt = {}
        self.semobj = {}
        for e in self.ENG:
            self.sems[e] = nc.alloc_semaphore(name=f"s_{e}")
            self.cnt[e] = 0
            self.semobj[("c", e)] = self.sems[e]
        self.dma_sems = {}
        self.dma_cnt = {}
        self.dma_rr = {}
        for q in ("sp", "pool"):
            self.dma_sems[q] = [nc.alloc_semaphore(name=f"d_{q}{i}") for i in range(n_dma_sems)]
            self.dma_cnt[q] = [0] * n_dma_sems
            self.dma_rr[q] = 0
            for i, sm in enumerate(self.dma_sems[q]):
                self.semobj[("d", q, i)] = sm
        self.waited = {e: {} for e in self.ENG}
        self.n_wait = 0

    def _wait(self, e, key, val):
        if key == ("c", "pe") and e == "pe":
            return
        w = self.waited[e]
        if w.get(key, 0) >= val:
            return
        w[key] = val
        self.eng[e].wait_ge(self.semobj[key], val)
        self.n_wait += 1

    def _deps(self, e, reads, writes):
        for b in reads:
            if b.w is not None:
                self._wait(e, *b.w)
        for b in writes:
            if b.w is not None:
                self._wait(e, *b.w)
            for ev in b.r:
                self._wait(e, *ev)

    def _post(self, ev, reads, writes):
        for b in writes:
            b.w = ev
            b.r = []
        for b in reads:
            if b in writes:
                continue
            b.r.append(ev)
            if len(b.r) > 8:
                best = {}
                for k, v in b.r:
                    if best.get(k, 0) < v:
                        best[k] = v
                b.r = list(best.items())

    def op(self, e, fn, reads=(), writes=()):
        self._deps(e, reads, writes)
        ins = fn(self.eng[e])
        self.cnt[e] += 1
        ins.then_inc(self.sems[e], 1)
        ev = (("c", e), self.cnt[e])
        self._post(ev, reads, writes)
        return ev

    def dma(self, q, out, in_, reads=(), writes=()):
        self._deps(q, reads, writes)
        i = self.dma_rr[q]
        self.dma_rr[q] = (i + 1) % len(self.dma_sems[q])
        key = ("d", q, i)
        if self.dma_cnt[q][i] > 0:
            self._wait(q, key, self.dma_cnt[q][i])
        self.dma_cnt[q][i] += 16
        ins = self.eng[q].dma_start(out=out, in_=in_)
        ins.then_inc(self.dma_sems[q][i], 16)
        ev = (key, self.dma_cnt[q][i])
        self._post(ev, reads, writes)
        return ev

    def drain_dma(self, q, e=None):
        e = e or q
        for i, c in enumerate(self.dma_cnt[q]):
            if c:
                self._wait(e, ("d", q, i), c)

    def barrier(self):
        for e in self.ENG:
            for e2 in self.ENG:
                if e2 != e and self.cnt[e2]:
                    self._wait(e, ("c", e2), self.cnt[e2])
            for q in self.dma_sems:
                self.drain_dma(q, e)


def interleave(ga, gb, ratio):
    da = db = False
    acc = 0.0
    while not (da and db):
        acc += ratio
        while acc >= 1.0 and not da:
            acc -= 1.0
            try:
                next(ga)
            except StopIteration:
                da = True
        if not db:
            try:
                next(gb)
            except StopIteration:
                db = True
        if db and not da:
            for _ in ga:
                pass
            da = True


class Prog:
    def __init__(self):
        self.nc = bass.Bass("TRN2", target_bir_lowering=False)
        self.s = Sched(self.nc)
        self.stack = ExitStack()
        nc = self.nc
        mk = lambda n: [nc.alloc_psum_tensor(f"{n}{i}", [128, 512], F32) for i in range(2)]
        self.ps_proj, self.ps_m, self.ps_o, self.ps_sd = mk("ps_proj"), mk("ps_m"), mk("ps_o"), mk("ps_sd")
        self.b_proj = [Buf("pj0"), Buf("pj1")]
        self.b_m = [Buf("m0"), Buf("m1")]
        self.b_o = [Buf("o0"), Buf("o1")]
        self.b_sd = [Buf("sd0"), Buf("sd1")]
        self.mcount = 0
        self.pcount = 0

    def mbank(self):
        i = self.mcount % 2
        self.mcount += 1
        return i

    def pbank(self):
        i = self.pcount % 2
        self.pcount += 1
        return i

    def sb(self, name, shape, dt, persist=False):
        self.nalloc = getattr(self, "nalloc", 0) + 1
        name = f"{name}_{self.nalloc}"
        if persist:
            return self.nc.alloc_sbuf_tensor(name, list(shape), dt)
        return self.stack.enter_context(self.nc.sbuf_tensor(name, list(shape), dt))

    def new_phase(self):
        self.s.barrier()
        self.stack.close()
        self.stack = ExitStack()

    def din(self, name, shape, dt=F32):
        return self.nc.dram_tensor(name, list(shape), dt, kind="ExternalInput").ap()

    def dout(self, name, shape, dt=F32):
        return self.nc.dram_tensor(name, list(shape), dt, kind="ExternalOutput").ap()

    def dscr(self, name, shape, dt=F32):
        return self.nc.dram_tensor(name, list(shape), dt, kind="Internal").ap()


class WStream:
    def __init__(self, P, tiles, bufs, bbufs):
        self.P, self.tiles, self.bufs, self.bb = P, tiles, bufs, bbufs
        self.nw = len(bufs) if bufs else 0
        self.issued = 0

    def get(self, k):
        while self.issued < min(len(self.tiles), k + self.nw):
            i = self.issued % self.nw
            self.P.s.dma("pool", self.bufs[i][:, :, :], self.tiles[self.issued], writes=[self.bb[i]])
            self.issued += 1
        return k % self.nw


class Exchanger:
    def __init__(self, P, name, F, groups, xmode, nslots=4):
        self.P, self.F, self.groups, self.xmode, self.n = P, F, groups, xmode, nslots
        self.xin = [P.dscr(f"{name}_xin{k}", [128, F]) for k in range(nslots)]
        self.xout = [P.dscr(f"{name}_xout{k}", [256, F]) for k in range(nslots)]
        self.b_in = [Buf(f"xin{k}") for k in range(nslots)]
        self.b_out = [Buf(f"xout{k}") for k in range(nslots)]
        self.cnt = 0

    def send(self, S_f, b_Sf, Sp_f, b_Spf):
        s = self.P.s
        k = self.cnt % self.n
        self.cnt += 1
        s.dma("sp", self.xin[k], S_f, reads=(b_Sf if isinstance(b_Sf, list) else [b_Sf]), writes=[self.b_in[k]])
        if self.xmode == "cc":
            s.op("pool", lambda e: e.collective_compute("AllGather", ALU.bypass, replica_groups=self.groups,
                                                        ins=[self.xin[k]], outs=[self.xout[k]]),
                 [self.b_in[k]], [self.b_out[k]])
            s.dma("sp", Sp_f, self.xout[k][0:128, :], reads=[self.b_out[k]], writes=[b_Spf])
        else:
            s.dma("sp", Sp_f, self.xin[k], reads=[self.b_in[k]], writes=[b_Spf])

    def recv(self, Sp_f, b_Spf, Sp_b, b_Spb, flag, b_flag):
        self.P.s.op("dve", lambda e: e.tensor_scalar(Sp_b, Sp_f, flag, None, ALU.mult), [b_Spf, b_flag], [b_Spb])


def emit_layer0(P, C, xT, w_in, vecs, w_out, lnv, yT_d, rT_d, hT_out, groups, xmode, H=64, do_out=True):
    nc, s = P.nc, P.s
    ps_proj, b_proj, ps_m, b_m, ps_o, b_o, ps_sd, b_sd = P.ps_proj, P.b_proj, P.ps_m, P.b_m, P.ps_o, P.b_o, P.ps_sd, P.b_sd
    cst_f, ident, ones, eps_c, b_cst, flag, b_flag = C["cst_f"], C["ident"], C["ones128"], C["eps_c"], C["b_cst"], C["flag"], C["b_flag"]
    cmask = cst_f[0:CH, 256:256 + CH]
    smask = cst_f[:, 256 + CH:256 + CH + NTOK]
    mbank = P.mbank

    vec = P.sb("vec", [128, 5, 64], F32)
    lb = P.sb("lb", [128, 64], F32)
    oml = P.sb("oml", [128, 64], F32)
    noml = P.sb("noml", [128, 64], F32)
    lbt = P.sb("lbt", [128, 3, 64], F32)
    b_vec = Buf("vec")
    s.dma("sp", vec[:, :, :], vecs, writes=[b_vec])
    s.op("act", lambda e: e.activation(lbt[:, :, :], vec[:, 2:5, :], AF.Exp), [b_vec], [b_vec])
    s.op("dve", lambda e: e.tensor_add(lb[:, :], lbt[:, 0, :], lbt[:, 1, :]), [b_vec], [b_vec])
    s.op("dve", lambda e: e.tensor_add(lb[:, :], lb[:, :], lbt[:, 2, :]), [b_vec], [b_vec])
    s.op("dve", lambda e: e.reciprocal(lb[:, :], lb[:, :]), [b_vec], [b_vec])
    s.op("dve", lambda e: e.tensor_mul(lb[:, :], lb[:, :], lbt[:, 0, :]), [b_vec], [b_vec])
    s.op("dve", lambda e: e.tensor_scalar(oml[:, :], lb[:, :], -1.0, 1.0, ALU.mult, ALU.add), [b_vec], [b_vec])
    s.op("dve", lambda e: e.tensor_scalar(noml[:, :], lb[:, :], 1.0, -1.0, ALU.mult, ALU.add), [b_vec], [b_vec])

    hT = P.sb("hT", [128, KC, NTOK], BF16)
    b_hTg = [Buf(f"hT{g}") for g in range(8)]
    WS = WStream(P, [w_in[h, p] for h in range(H) for p in range(4)], None, None)
    for g in range(8):
        s.dma("pool", hT[:, g * 4:(g + 1) * 4, :], xT[g * 4:(g + 1) * 4].rearrange("k p t -> p k t"), writes=[b_hTg[g]])

    NW = 4
    wt = [P.sb(f"wt{i}", [128, KC, 128], BF16) for i in range(NW)]
    b_wt = [Buf(f"wt{i}") for i in range(NW)]
    q_f = P.sb("q_f", [128, NTOK], F32); b_qf = Buf("q_f")
    sg = P.sb("sg", [128, NTOK], F32); b_sg = Buf("sg")
    k_f = P.sb("k_f", [128, NTOK], F32); b_kf = Buf("k_f")
    lg = P.sb("lg", [128, NTOK], F32); b_lg = Buf("lg")
    bb = P.sb("bb", [128, NTOK], F32); b_bb = Buf("bb")
    e2 = P.sb("e2", [128, NTOK], F32); b_e2 = Buf("e2")
    vT = P.sb("vT", [128, NTOK], BF16); b_vT = Buf("vT")
    kdT = P.sb("kdT", [128, NTOK], BF16); b_kdT = Buf("kdT")
    pt = P.sb("pt", [128, NCH + 1], F32); b_pt = Buf("pt")
    zer = P.sb("zer", [128, NCH], F32)
    s.op("dve", lambda e: e.memset(pt[:, :], 1.0), [], [b_pt])
    s.op("dve", lambda e: e.memset(zer[:, :], 0.0), [], [b_pt])
    e1 = [P.sb(f"e1_{i}", [128, NTOK], F32) for i in range(2)]; b_e1 = [Buf("e1_0"), Buf("e1_1")]
    qd = [P.sb(f"qd{i}", [128, NTOK], BF16) for i in range(2)]; b_qd = [Buf("qd0"), Buf("qd1")]
    qg = [P.sb(f"qg{i}", [128, NTOK], BF16) for i in range(3)]; b_qg = [Buf(f"qg{i}") for i in range(3)]
    ki = [P.sb(f"ki{i}", [128, NTOK], BF16) for i in range(2)]; b_ki = [Buf("ki0"), Buf("ki1")]
    kdt = [P.sb(f"kdt{i}", [128, NCH, 128], BF16) for i in range(2)]; b_kdt = [Buf("kdt0"), Buf("kdt1")]
    vt = [P.sb(f"vt{i}", [128, NCH, 128], BF16) for i in range(2)]; b_vt = [Buf("vt0"), Buf("vt1")]
    a_sb = [P.sb(f"a_sb{i}", [128, NCH, CH], BF16) for i in range(2)]; b_asb = [Buf("a0"), Buf("a1")]
    sz = [P.sb(f"sz{i}", [128, NTOK], F32) for i in range(3)]; b_sz = [Buf("sz0"), Buf("sz1"), Buf("sz2")]
    S_f = P.sb("S_f", [128, 128], F32); b_Sf = Buf("S_f")
    S_b = [P.sb(f"S_b{i}", [128, 128], BF16) for i in range(2)]; b_Sb = [Buf("S_b0"), Buf("S_b1")]
    Sp_f = [P.sb(f"Sp_f{i}", [128, 128], F32) for i in range(2)]; b_Spf = [Buf("spf0"), Buf("spf1")]
    Sp_b = [P.sb(f"Sp_b{i}", [128, 128], BF16) for i in range(2)]; b_Spb = [Buf("spb0"), Buf("spb1")]
    o_f = [P.sb(f"o_f{i}", [128, NTOK], F32) for i in range(2)]; b_of = [Buf("of0"), Buf("of1")]
    sq = [P.sb(f"sq{i}", [128, TT], BF16) for i in range(3)]; b_sq = [Buf("sq0"), Buf("sq1"), Buf("sq2")]
    rstd = [P.sb(f"rstd{i}", [128, TT], F32) for i in range(2)]; b_rstd = [Buf("rs0"), Buf("rs1")]
    t1 = [P.sb(f"t1_{i}", [128, TT], F32) for i in range(2)]; b_t1 = [Buf("t10"), Buf("t11")]
    y_b = [P.sb(f"y_b{i}", [128, NTOK], BF16) for i in range(2)]; b_yb = [Buf("yb0"), Buf("yb1")]
    b_yTd = Buf("yT_d")
    X = Exchanger(P, "x0", 128, groups, xmode)
    wcount = [0]
    tsl = lambda tt: slice(tt * TT, (tt + 1) * TT)

    WS.bufs, WS.bb, WS.nw = wt, b_wt, NW

    def proj(wi, col, evac):
        for tt in range(NTT):
            pb = P.pbank()
            for kc in range(KC):
                s.op("pe", lambda e: e.matmul(ps_proj[pb][:, 0:TT], wt[wi][:, kc, :],
                                              hT[:, kc, tsl(tt)], start=(kc == 0), stop=(kc == KC - 1)),
                     [b_wt[wi], b_hTg[kc // 4]], [b_proj[pb]])
                if kc % 8 == 7:
                    yield
            evac(tt, ps_proj[pb][:, 0:TT], b_proj[pb])

    def stage1(h):
        par = h % 2
        hc = slice(h, h + 1)
        def ev_q(tt, ps, bps):
            s.op("act", lambda e: e.activation(q_f[:, tsl(tt)], ps, AF.Silu), [bps], [b_qf])
        yield from proj(WS.get(4 * h), 0, ev_q)

        def ev_f(tt, ps, bps):
            s.op("act", lambda e: e.activation(sg[:, tsl(tt)], ps, AF.Sigmoid, bias=vec[:, 0, hc], scale=1.0),
                 [bps, b_vec], [b_sg])
        yield from proj(WS.get(4 * h + 1), 128, ev_f)
        def ev_i(tt, ps, bps):
            s.op("act", lambda e: e.copy(vT[:, tsl(tt)], ps), [bps], [b_vT])
        gi = proj(WS.get(4 * h + 2), 0, ev_i)

        def chain():
            s.op("dve", lambda e: e.tensor_scalar(k_f[:, :], sg[:, :], noml[:, hc], oml[:, hc], ALU.mult, ALU.add),
                 [b_sg, b_vec], [b_kf])
            s.op("act", lambda e: e.activation(lg[:, :], sg[:, :], AF.Ln, bias=lb[:, hc], scale=oml[:, hc]),
                 [b_sg, b_vec], [b_lg])
            yield
            s.op("dve", lambda e: e.tensor_tensor_scan(bb[:, :], smask, lg[:, :], 0.0, ALU.mult, ALU.add),
                 [b_lg, b_cst], [b_bb])
            yield
            s.op("act", lambda e: e.activation(e1[par][:, :], bb[:, :], AF.Exp), [b_bb], [b_e1[par]])
            s.op("act", lambda e: e.activation(e2[:, :], bb[:, :], AF.Exp, scale=-1.0), [b_bb], [b_e2])
            yield
            e1c = e1[par][:, :].rearrange("p (c j) -> p c j", j=CH)
            s.op("dve", lambda e: e.tensor_copy(lg[:, 0:NCH], e1c[:, :, CH - 1]), [b_e1[par], b_lg], [b_lg])
            s.op("dve", lambda e: e.tensor_tensor_scan(pt[:, 1:NCH + 1], lg[:, 0:NCH], zer[:, :], 1.0, ALU.mult, ALU.add),
                 [b_lg, b_pt], [b_pt])
            s.op("dve", lambda e: e.tensor_tensor(q_f[:, :], q_f[:, :], e1[par][:, :], ALU.mult), [b_qf, b_e1[par]], [b_qf])
            yield
            s.op("act", lambda e: e.copy(qd[par][:, :], q_f[:, :]), [b_qf], [b_qd[par]])
            g3 = h % 3
            s.op("dve", lambda e: e.tensor_tensor(qg[g3][:, :].rearrange("p (c j) -> p c j", j=CH),
                                                  q_f[:, :].rearrange("p (c j) -> p c j", j=CH),
                                                  pt[:, 0:NCH].unsqueeze(2).to_broadcast([128, NCH, CH]), ALU.mult),
                 [b_qf, b_pt], [b_qg[g3]])
            yield
            s.op("dve", lambda e: e.tensor_tensor(e2[:, :], k_f[:, :], e2[:, :], ALU.mult), [b_kf, b_e2], [b_e2])
            yield
            s.op("act", lambda e: e.copy(ki[par][:, :], e2[:, :]), [b_e2], [b_ki[par]])
            e1l = e1c[:, :, CH - 1:CH].to_broadcast([128, NCH, CH])
            s.op("dve", lambda e: e.tensor_tensor(kdT[:, :].rearrange("p (c j) -> p c j", j=CH),
                                                  e2[:, :].rearrange("p (c j) -> p c j", j=CH), e1l, ALU.mult),
                 [b_e2, b_e1[par]], [b_kdT])
            yield


        for _ in chain():
            yield
            for _k in range(2):
                if next(gi, "done") != "done":
                    yield
        for _ in gi:
            yield

        def ev_z(tt, ps, bps):
            s.op("act", lambda e: e.activation(sz[h % 3][:, tsl(tt)], ps, AF.Silu), [bps], [b_sz[h % 3]])
        yield from proj(WS.get(4 * h + 3), 128, ev_z)
        for g in range(3):
            mb = mbank()
            for j in range(4):
                c = g * 4 + j
                cs = slice(c * CH, (c + 1) * CH)
                s.op("pe", lambda e: e.matmul(ps_m[mb][0:CH, j * 128:j * 128 + CH], ki[par][:, cs], qd[par][:, cs],
                                              start=True, stop=True), [b_ki[par], b_qd[par]], [b_m[mb]])
            s.op("dve", lambda e: e.tensor_tensor(a_sb[par][0:CH, g * 4:(g + 1) * 4, :],
                                                  ps_m[mb][0:CH, :].rearrange("p (c v) -> p c v", v=128)[:, :, 0:CH],
                                                  cmask.unsqueeze(1).to_broadcast([CH, 4, CH]), ALU.mult),
                 [b_m[mb], b_cst], [b_asb[par]])
        yield
        for g in range(3):
            mb = mbank()
            pv = ps_m[mb][:, :].bitcast(BF16)
            for j in range(4):
                c = g * 4 + j
                s.op("pe", lambda e: e.transpose(pv[0:CH, j * 128:(j + 1) * 128], kdT[:, c * CH:(c + 1) * CH], ident[:, :]),
                     [b_kdT, b_cst], [b_m[mb]])
            s.op("act", lambda e: e.copy(kdt[par][0:CH, g * 4:(g + 1) * 4, :],
                                         pv[0:CH, 0:512].rearrange("p (c v) -> p c v", v=128)),
                 [b_m[mb]], [b_kdt[par]])
        yield
        for g in range(3):
            mb = mbank()
            pv = ps_m[mb][:, :].bitcast(BF16)
            for j in range(4):
                c = g * 4 + j
                s.op("pe", lambda e: e.transpose(pv[0:CH, j * 128:(j + 1) * 128], vT[:, c * CH:(c + 1) * CH], ident[:, :]),
                     [b_vT, b_cst], [b_m[mb]])
            s.op("dve", lambda e: e.tensor_copy(vt[par][0:CH, g * 4:(g + 1) * 4, :],
                                                pv[0:CH, 0:512].rearrange("p (c v) -> p c v", v=128)),
                 [b_m[mb]], [b_vt[par]])
        yield

    def finalize(h):
        par = h % 2
        hc = slice(h, h + 1)
        g3 = h % 3
        X.recv(Sp_f[par][:, :], b_Spf[par], Sp_b[par][:, :], b_Spb[par], flag[:, 0:1], b_flag)
        for tt in range(NTT):
            mb = mbank()
            s.op("pe", lambda e: e.matmul(ps_m[mb][:, 0:TT], Sp_b[par][:, :], qg[g3][:, tsl(tt)], start=True, stop=True),
                 [b_Spb[par], b_qg[g3]], [b_m[mb]])
            s.op("dve", lambda e: e.tensor_tensor(o_f[par][:, tsl(tt)], o_f[par][:, tsl(tt)], ps_m[mb][:, 0:TT], ALU.add),
                 [b_of[par], b_m[mb]], [b_of[par]])
            s.op("act", lambda e: e.activation(sq[tt][:, :], o_f[par][:, tsl(tt)], AF.Square), [b_of[par]], [b_sq[tt]])
            yield
        for tt in range(NTT):
            ob = tt % 2
            mb = mbank()
            s.op("pe", lambda e: e.matmul(ps_m[mb][:, 0:TT], ones[:, :], sq[tt][:, :], start=True, stop=True),
                 [b_sq[tt], b_cst], [b_m[mb]])
            s.op("act", lambda e: e.activation(rstd[ob][:, :], ps_m[mb][:, 0:TT], AF.Ln, bias=eps_c[:, 0:1], scale=1.0),
                 [b_m[mb], b_cst], [b_rstd[ob]])
            s.op("act", lambda e: e.activation(rstd[ob][:, :], rstd[ob][:, :], AF.Exp, scale=-0.5),
                 [b_rstd[ob]], [b_rstd[ob]])
            s.op("dve", lambda e: e.scalar_tensor_tensor(t1[ob][:, :], o_f[par][:, tsl(tt)], vec[:, 1, hc], rstd[ob][:, :], ALU.mult, ALU.mult),
                 [b_of[par], b_rstd[ob], b_vec], [b_t1[ob]])
            s.op("dve", lambda e: e.tensor_tensor(y_b[par][:, tsl(tt)], t1[ob][:, :], sz[h % 3][:, tsl(tt)], ALU.mult),
                 [b_t1[ob], b_sz[h % 3]], [b_yb[par]])
            yield
        s.dma("sp", yT_d[h], y_b[par][:, :], reads=[b_yb[par]], writes=[b_yTd])

    def stage2(h):
        par = h % 2
        s.op("dve", lambda e: e.memset(S_f[:, :], 0.0), [], [b_Sf])
        s.op("dve", lambda e: e.memset(S_b[0][:, :], 0.0), [], [b_Sb[0]])
        for tt in range(NTT):
            ob = tt % 2
            for j in range(CPT):
                c = tt * CPT + j
                s.op("pe", lambda e: e.matmul(ps_sd[ob][:, j * 128:(j + 1) * 128], kdt[par][0:CH, c, :], vt[par][0:CH, c, :],
                                              start=True, stop=True), [b_kdt[par], b_vt[par]], [b_sd[ob]])
            for j in range(CPT):
                c = tt * CPT + j
                cs = slice(c * CH, (c + 1) * CH)
                osl = ps_o[ob][:, j * CH:(j + 1) * CH]
                sbi = c % 2
                s.op("pe", lambda e: e.matmul(osl, vt[par][0:CH, c, :], a_sb[par][0:CH, c, :], start=True, stop=False),
                     [b_vt[par], b_asb[par]], [b_o[ob]])
                s.op("pe", lambda e: e.matmul(osl, S_b[sbi][:, :], qd[par][:, cs], start=False, stop=True),
                     [b_Sb[sbi], b_qd[par]], [b_o[ob]])
                gl = e1[par][:, c * CH + CH - 1:c * CH + CH]
                sd = ps_sd[ob][:, j * 128:(j + 1) * 128]
                if c < NCH - 1:
                    s.op("dve", lambda e: e.scalar_tensor_tensor(S_b[1 - sbi][:, :], S_f[:, :], gl, sd, ALU.mult, ALU.add),
                         [b_Sf, b_sd[ob], b_e1[par]], [b_Sb[1 - sbi]])
                s.op("dve", lambda e: e.scalar_tensor_tensor(S_f[:, :], S_f[:, :], gl, sd, ALU.mult, ALU.add),
                     [b_Sf, b_sd[ob], b_e1[par]], [b_Sf])
                yield
            s.op("act", lambda e: e.copy(o_f[par][:, tsl(tt)], ps_o[ob][:, 0:TT]), [b_o[ob]], [b_of[par]])
            if tt == 1 and h > 0:
                yield from finalize(h - 1)
            yield
        X.send(S_f[:, :], b_Sf, Sp_f[par][:, :], b_Spf[par])

    for _ in stage1(0):
        pass
    for h in range(H):
        if h + 1 < H:
            interleave(stage1(h + 1), stage2(h), 2.7)
        else:
            for _ in stage2(h):
                pass
    for _ in finalize(H - 1):
        pass
    if not do_out:
        return
    P.new_phase()
    out_ln_phase(P, C, yT_d, w_out, lnv, xT, rT_d, hT_out, b_yTd)


def emit_layer1(P, C, xT, b_xT, w_in, wg1, wg2, vecs, w_out, lnv, yu_d, rs_d, rT_d, hT_out, groups, xmode, H=8, do_out=True):
    nc, s = P.nc, P.s
    ps_proj, b_proj, ps_m, b_m, ps_o, b_o, ps_sd, b_sd = P.ps_proj, P.b_proj, P.ps_m, P.b_m, P.ps_o, P.b_o, P.ps_sd, P.b_sd
    cst_f, ident, ones, eps_c, b_cst, flag, b_flag = C["cst_f"], C["ident"], C["ones1024"], C["eps_c"], C["b_cst"], C["flag"], C["b_flag"]
    cmask = cst_f[0:CH, 256:256 + CH]
    smask = cst_f[:, 256 + CH:256 + CH + NTOK]
    mbank = P.mbank
    ND, NV = 4, 8
    QSCALE = 512.0 ** -0.5

    vec = P.sb("vec1", [128, 96], F32)
    nbg = P.sb("nbg", [128, 32], F32)
    b_vec = Buf("vec1")
    s.dma("sp", vec[:, :], vecs, writes=[b_vec])
    s.op("dve", lambda e: e.tensor_scalar(nbg[:, :], vec[:, 64:96], -1.0, None, ALU.mult), [b_vec], [b_vec])

    hT = P.sb("hT1", [128, KC, NTOK], BF16)
    b_hTg = [Buf(f"hT1_{g}") for g in range(8)]
    for g in range(8):
        s.dma("pool", hT[:, g * 4:(g + 1) * 4, :], xT[g * 4:(g + 1) * 4].rearrange("k p t -> p k t"), reads=[b_xT], writes=[b_hTg[g]])
    wg1b = P.sb("wg1b", [128, KC, 16], BF16)
    wg2h = [P.sb(f"wg2h{i}", [16, 512], BF16) for i in range(2)]
    b_wg2 = [Buf("wg2h0"), Buf("wg2h1")]
    u_b = P.sb("u_b", [16, NTOK], BF16)
    b_g = Buf("gatew")
    s.dma("pool", wg1b[:, :, :], wg1, writes=[b_g])

    NW = 3
    wt = [P.sb(f"wu{i}", [128, KC, 128], BF16) for i in range(NW)]
    b_wt = [Buf(f"wu{i}") for i in range(NW)]
    sp_t = P.sb("sp_t", [128, NTOK], F32); b_sp = Buf("sp")
    bs_t = P.sb("bs_t", [128, NTOK], F32); b_bs = Buf("bs")
    e1 = P.sb("e1", [128, NTOK], F32); b_e1 = Buf("e1")
    e2 = P.sb("e2g", [128, NTOK], F32); b_e2 = Buf("e2")
    qd = P.sb("qd", [128, ND, NTOK], BF16); b_qd = Buf("qd")
    qg = P.sb("qg", [128, ND, NTOK], BF16); b_qg = Buf("qg")
    ki = P.sb("ki", [128, ND, NTOK], BF16); b_ki = Buf("ki")
    kdT = P.sb("kdT1", [128, NTOK], BF16); b_kdT = Buf("kdT")
    kdt = P.sb("kdt", [128, NCH, ND * 128], BF16); b_kdt = Buf("kdt")
    a_sb = P.sb("a_sb", [128, NCH, CH], BF16); b_asb = Buf("a_sb")
    glast = P.sb("glast", [128, ND, NCH], F32); b_gl = Buf("glast")
    pt = P.sb("pt1", [128, NCH + 1], F32); b_pt = Buf("pt1")
    zer = P.sb("zer1", [128, NCH], F32)
    s.op("dve", lambda e: e.memset(pt[:, :], 1.0), [], [b_pt])
    s.op("dve", lambda e: e.memset(zer[:, :], 0.0), [], [b_pt])
    vT = P.sb("vT1", [128, NTOK], BF16); b_vT = Buf("vT")
    vt_ = [P.sb(f"vtok{i}", [128, NCH, 128], BF16) for i in range(2)]; b_vt = [Buf("vt0"), Buf("vt1")]
    sz = [P.sb(f"sz1_{i}", [128, NTOK], F32) for i in range(3)]; b_sz = [Buf("sz0"), Buf("sz1"), Buf("sz2")]
    S_f = P.sb("S_f1", [128, ND * 128], F32); b_Sf = [Buf(f"S_f{d}") for d in range(ND)]
    S_b = [P.sb(f"S_b1_{i}", [128, ND * 128], BF16) for i in range(2)]
    b_Sb = [[Buf(f"S_b{i}_{d}") for d in range(ND)] for i in range(2)]
    Sp_f = [P.sb(f"Sp_f1_{i}", [128, ND * 128], F32) for i in range(2)]; b_Spf = [Buf("spf0"), Buf("spf1")]
    Sp_b = [P.sb(f"Sp_b1_{i}", [128, ND * 128], BF16) for i in range(2)]; b_Spb = [Buf("spb0"), Buf("spb1")]
    o_f = [P.sb(f"o_f1_{i}", [128, NTOK], F32) for i in range(2)]; b_of = [Buf("of0"), Buf("of1")]
    sq = [P.sb(f"sq1_{i}", [128, TT], BF16) for i in range(3)]; b_sq = [Buf("sq0"), Buf("sq1"), Buf("sq2")]
    yu_t = [P.sb(f"yu{i}", [128, TT], F32) for i in range(2)]; b_yu = [Buf("yu0"), Buf("yu1")]
    ssacc = P.sb("ssacc", [128, NTOK], F32); b_ss = Buf("ssacc")
    b_yud = Buf("yu_d")
    X = Exchanger(P, "x1", ND * 128, groups, xmode)
    tsl = lambda tt: slice(tt * TT, (tt + 1) * TT)
    dsl = lambda dt: slice(dt * 128, (dt + 1) * 128)

    for tt in range(NTT):
        pb = P.pbank()
        for kc in range(KC):
            s.op("pe", lambda e: e.matmul(ps_proj[pb][0:16, 0:TT], wg1b[:, kc, :], hT[:, kc, tsl(tt)],
                                          start=(kc == 0), stop=(kc == KC - 1)), [b_g, b_hTg[kc // 4]], [b_proj[pb]])
        s.op("act", lambda e: e.copy(u_b[0:16, tsl(tt)], ps_proj[pb][0:16, 0:TT]), [b_proj[pb]], [b_g])

    WS = WStream(P, [w_in[hd, ti] for hd in range(H) for ti in range(24)], wt, b_wt)

    def load_w(hd, ti):
        return WS.get(hd * 24 + ti)

    def proj(wi, evac):
        for tt in range(NTT):
            pb = P.pbank()
            for kc in range(KC):
                s.op("pe", lambda e: e.matmul(ps_proj[pb][:, 0:TT], wt[wi][:, kc, :], hT[:, kc, tsl(tt)],
                                              start=(kc == 0), stop=(kc == KC - 1)), [b_wt[wi], b_hTg[kc // 4]], [b_proj[pb]])
                if kc % 8 == 7:
                    yield
            evac(tt, ps_proj[pb][:, 0:TT], b_proj[pb])

    def stageA(hd, inject=None):
        hp = hd % 2
        pending_tr = []
        if hd == 0:
            s.dma("pool", wg2h[0][:, :], wg2[:, 0:512], writes=[b_wg2[0]])
        if hd + 1 < H:
            s.dma("pool", wg2h[1 - hp][:, :], wg2[:, (hd + 1) * 512:(hd + 2) * 512], writes=[b_wg2[1 - hp]])
        for dt in range(ND):
            gdt = hd * ND + dt
            wq = load_w(hd, 2 * dt)
            for tt in range(NTT):
                pb = P.pbank()
                s.op("pe", lambda e: e.matmul(ps_proj[pb][:, 0:TT], wg2h[hp][0:16, dsl(dt)], u_b[0:16, tsl(tt)],
                                              start=True, stop=True), [b_g, b_wg2[hp]], [b_proj[pb]])
                s.op("act", lambda e: e.activation(sp_t[:, tsl(tt)], ps_proj[pb][:, 0:TT], AF.Exp, bias=nbg[:, gdt:gdt + 1], scale=-1.0),
                     [b_proj[pb], b_vec], [b_sp])
            s.op("act", lambda e: e.activation(sp_t[:, :], sp_t[:, :], AF.Ln, bias=1.0, scale=1.0), [b_sp], [b_sp])
            s.op("dve", lambda e: e.tensor_tensor_scan(bs_t[:, :], smask, sp_t[:, :], 0.0, ALU.mult, ALU.add),
                 [b_sp, b_cst], [b_bs])
            s.op("act", lambda e: e.activation(e1[:, :], bs_t[:, :], AF.Exp, scale=-1.0 / 16), [b_bs], [b_e1])
            s.op("act", lambda e: e.activation(e2[:, :], bs_t[:, :], AF.Exp, scale=1.0 / 16), [b_bs], [b_e2])
            e1c = e1[:, :].rearrange("p (c j) -> p c j", j=CH)
            s.op("dve", lambda e: e.tensor_copy(glast[:, dt, :], e1c[:, :, CH - 1]), [b_e1], [b_gl])
            s.op("dve", lambda e: e.tensor_tensor_scan(pt[:, 1:NCH + 1], glast[:, dt, :], zer[:, :], 1.0, ALU.mult, ALU.add),
                 [b_gl, b_pt], [b_pt])
            yield

            def ev_q(tt, ps, bps):
                s.op("dve", lambda e: e.scalar_tensor_tensor(bs_t[:, tsl(tt)], ps, QSCALE, e1[:, tsl(tt)], ALU.mult, ALU.mult),
                     [bps, b_e1], [b_bs])
            yield from proj(wq, ev_q)
            while pending_tr:
                pending_tr.pop(0)()
            if dt == 0 and inject is not None:
                inject()
            s.op("act", lambda e: e.copy(qd[:, dt, :], bs_t[:, :]), [b_bs], [b_qd])
            s.op("dve", lambda e: e.tensor_tensor(qg[:, dt, :].rearrange("p (c j) -> p c j", j=CH),
                                                  bs_t[:, :].rearrange("p (c j) -> p c j", j=CH),
                                                  pt[:, 0:NCH].unsqueeze(2).to_broadcast([128, NCH, CH]), ALU.mult),
                 [b_bs, b_pt], [b_qg])

            def ev_k(tt, ps, bps):
                s.op("dve", lambda e: e.tensor_tensor(sp_t[:, tsl(tt)], ps, e2[:, tsl(tt)], ALU.mult), [bps, b_e2], [b_sp])
            wk = load_w(hd, 2 * dt + 1)
            yield from proj(wk, ev_k)
            s.op("act", lambda e: e.copy(ki[:, dt, :], sp_t[:, :]), [b_sp], [b_ki])
            e1l = e1c[:, :, CH - 1:CH].to_broadcast([128, NCH, CH])
            s.op("dve", lambda e: e.tensor_tensor(kdT[:, :].rearrange("p (c j) -> p c j", j=CH),
                                                  sp_t[:, :].rearrange("p (c j) -> p c j", j=CH), e1l, ALU.mult),
                 [b_sp, b_e1], [b_kdT])
            def tr(dt=dt):
                for g in range(3):
                    mb = mbank()
                    pv = ps_m[mb][:, :].bitcast(BF16)
                    for j in range(4):
                        c = g * 4 + j
                        s.op("pe", lambda e: e.transpose(pv[0:CH, j * 128:(j + 1) * 128], kdT[:, c * CH:(c + 1) * CH], ident[:, :]),
                             [b_kdT, b_cst], [b_m[mb]])
                    s.op("act", lambda e: e.copy(kdt[0:CH, g * 4:(g + 1) * 4, dsl(dt)],
                                                 pv[0:CH, 0:512].rearrange("p (c v) -> p c v", v=128)),
                         [b_m[mb]], [b_kdt])
            pending_tr.append(tr)
            yield
        while pending_tr:
            pending_tr.pop(0)()
        for g in range(3):
            mb = mbank()
            for j in range(4):
                c = g * 4 + j
                cs = slice(c * CH, (c + 1) * CH)
                for dt in range(ND):
                    s.op("pe", lambda e: e.matmul(ps_m[mb][0:CH, j * 128:j * 128 + CH], ki[:, dt, cs], qd[:, dt, cs],
                                                  start=(dt == 0), stop=(dt == ND - 1)), [b_ki, b_qd], [b_m[mb]])
            s.op("dve", lambda e: e.tensor_tensor(a_sb[0:CH, g * 4:(g + 1) * 4, :],
                                                  ps_m[mb][0:CH, :].rearrange("p (c v) -> p c v", v=128)[:, :, 0:CH],
                                                  cmask.unsqueeze(1).to_broadcast([CH, 4, CH]), ALU.mult),
                 [b_m[mb], b_cst], [b_asb])
        yield

    def stageB(hd, v):
        par = v % 2
        wv = load_w(hd, 8 + 2 * v)

        def ev_v(tt, ps, bps):
            s.op("act", lambda e: e.copy(vT[:, tsl(tt)], ps), [bps], [b_vT])
        yield from proj(wv, ev_v)
        wz = load_w(hd, 8 + 2 * v + 1)

        def ev_z(tt, ps, bps):
            s.op("act", lambda e: e.activation(sz[v % 3][:, tsl(tt)], ps, AF.Silu), [bps], [b_sz[v % 3]])
        yield from proj(wz, ev_z)
        for g in range(3):
            mb = mbank()
            pv = ps_m[mb][:, :].bitcast(BF16)
            for j in range(4):
                c = g * 4 + j
                s.op("pe", lambda e: e.transpose(pv[0:CH, j * 128:(j + 1) * 128], vT[:, c * CH:(c + 1) * CH], ident[:, :]),
                     [b_vT, b_cst], [b_m[mb]])
            s.op("dve", lambda e: e.tensor_copy(vt_[par][0:CH, g * 4:(g + 1) * 4, :],
                                                pv[0:CH, 0:512].rearrange("p (c v) -> p c v", v=128)),
                 [b_m[mb]], [b_vt[par]])

    def finalize(hd, v):
        par = v % 2
        cc = hd * NV + v
        X.recv(Sp_f[par][:, :], b_Spf[par], Sp_b[par][:, :], b_Spb[par], flag[:, 0:1], b_flag)
        for tt in range(NTT):
            mb = mbank()
            for dt in range(ND):
                s.op("pe", lambda e: e.matmul(ps_m[mb][:, 0:TT], Sp_b[par][:, dsl(dt)], qg[:, dt, tsl(tt)],
                                              start=(dt == 0), stop=(dt == ND - 1)), [b_Spb[par], b_qg], [b_m[mb]])
            s.op("dve", lambda e: e.tensor_tensor(o_f[par][:, tsl(tt)], o_f[par][:, tsl(tt)], ps_m[mb][:, 0:TT], ALU.add),
                 [b_of[par], b_m[mb]], [b_of[par]])
            s.op("act", lambda e: e.activation(sq[tt][:, :], o_f[par][:, tsl(tt)], AF.Square), [b_of[par]], [b_sq[tt]])
            yield
        for tt in range(NTT):
            ob = tt % 2
            mb = mbank()
            s.op("pe", lambda e: e.matmul(ps_m[mb][:, 0:TT], ones[:, :], sq[tt][:, :], start=True, stop=True),
                 [b_sq[tt], b_cst], [b_m[mb]])
            if v == 0:
                s.op("dve", lambda e: e.tensor_copy(ssacc[:, tsl(tt)], ps_m[mb][:, 0:TT]), [b_m[mb]], [b_ss])
            else:
                s.op("dve", lambda e: e.tensor_tensor(ssacc[:, tsl(tt)], ssacc[:, tsl(tt)], ps_m[mb][:, 0:TT], ALU.add),
                     [b_m[mb], b_ss], [b_ss])
            s.op("dve", lambda e: e.scalar_tensor_tensor(yu_t[ob][:, :], o_f[par][:, tsl(tt)], vec[:, cc:cc + 1], sz[v % 3][:, tsl(tt)],
                                                         ALU.mult, ALU.mult), [b_of[par], b_sz[v % 3], b_vec], [b_yu[ob]])
            s.dma("sp", yu_d[cc][:, tsl(tt)], yu_t[ob][:, :], reads=[b_yu[ob]], writes=[b_yud])
            yield

    def stageC(hd, v):
        par = v % 2
        s.op("dve", lambda e: e.memset(S_f[:, :], 0.0), [], b_Sf)
        s.op("dve", lambda e: e.memset(S_b[0][:, :], 0.0), [], b_Sb[0])
        for tt in range(NTT):
            ob = tt % 2
            for j in range(CPT):
                c = tt * CPT + j
                cs = slice(c * CH, (c + 1) * CH)
                sb_ = c % 2
                sbi = c % 2
                for dt in range(ND):
                    s.op("pe", lambda e: e.matmul(ps_sd[sb_][:, dsl(dt)], kdt[0:CH, c, dsl(dt)],
                                                  vt_[par][0:CH, c, :], start=True, stop=True), [b_kdt, b_vt[par]], [b_sd[sb_]])
                osl = ps_o[ob][:, j * CH:(j + 1) * CH]
                s.op("pe", lambda e: e.matmul(osl, vt_[par][0:CH, c, :], a_sb[0:CH, c, :], start=True, stop=False),
                     [b_vt[par], b_asb], [b_o[ob]])
                for dt in range(ND):
                    s.op("pe", lambda e: e.matmul(osl, S_b[sbi][:, dsl(dt)], qd[:, dt, cs], start=False, stop=(dt == ND - 1)),
                         [b_Sb[sbi][dt], b_qd], [b_o[ob]])
                if c < NCH - 1:
                    for dt in range(ND):
                        s.op("dve", lambda e: e.scalar_tensor_tensor(S_b[1 - sbi][:, dsl(dt)], S_f[:, dsl(dt)], glast[:, dt, c:c + 1],
                                                                     ps_sd[sb_][:, dsl(dt)], ALU.mult, ALU.add),
                             [b_Sf[dt], b_sd[sb_], b_gl], [b_Sb[1 - sbi][dt]])
                for dt in range(ND):
                    s.op("dve", lambda e: e.scalar_tensor_tensor(S_f[:, dsl(dt)], S_f[:, dsl(dt)], glast[:, dt, c:c + 1],
                                                                 ps_sd[sb_][:, dsl(dt)], ALU.mult, ALU.add),
                         [b_Sf[dt], b_sd[sb_], b_gl], [b_Sf[dt]])
                yield
            s.op("act", lambda e: e.copy(o_f[par][:, tsl(tt)], ps_o[ob][:, 0:TT]), [b_o[ob]], [b_of[par]])
            if tt == 1 and v > 0:
                yield from finalize(hd, v - 1)
            yield
        X.send(S_f[:, :], b_Sf, Sp_f[par][:, :], b_Spf[par])

    def head_end(hd):
        s.op("act", lambda e: e.activation(ssacc[:, :], ssacc[:, :], AF.Ln, bias=eps_c[:, 0:1], scale=1.0), [b_ss, b_cst], [b_ss])
        s.op("act", lambda e: e.activation(ssacc[:, :], ssacc[:, :], AF.Exp, scale=-0.5), [b_ss], [b_ss])
        s.dma("sp", rs_d[hd], ssacc[:, :], reads=[b_ss], writes=[b_yud])

    def tail(hd):
        for _ in finalize(hd, NV - 1):
            pass
        head_end(hd)

    for hd in range(H):
        for _ in stageA(hd, inject=(lambda p=hd - 1: tail(p)) if hd > 0 else None):
            pass
        for _ in stageB(hd, 0):
            pass
        for v in range(NV):
            if v + 1 < NV:
                interleave(stageB(hd, v + 1), stageC(hd, v), 1.12)
            else:
                for _ in stageC(hd, v):
                    pass
    tail(H - 1)
    if not do_out:
        return
    P.new_phase()
    out_ln_phase(P, C, None, w_out, lnv, xT, rT_d, hT_out, None, yu=(yu_d, rs_d, b_yud))


def out_ln_phase(P, C, yT_d, w_out, lnv, resT, rT_d, hT_out, b_yTd, yu=None):
    nc, s = P.nc, P.s
    cst_f, b_cst = C["cst_f"], C["b_cst"]
    ps_acc, b_acc = P.ps_proj, P.b_proj
    ps_st = [P.ps_m[0], P.ps_m[1], P.ps_o[0], P.ps_o[1], P.ps_sd[0], P.ps_sd[1]]
    b_st = [P.b_m[0], P.b_m[1], P.b_o[0], P.b_o[1], P.b_sd[0], P.b_sd[1]]
    lnp = P.sb("lnp", [128, 2, KC], F32); b_lnp = Buf("lnp")
    mean = P.sb("mean", [128, NTOK], F32); b_mean = Buf("mean")
    rstd = P.sb("rstdl", [128, NTOK], F32); b_rstd = Buf("rstdl")
    lneps = P.sb("lneps", [128, 1], F32)
    onesd = P.sb("onesd", [128, 128], BF16)
    res = [P.sb(f"res{i}", [128, NTOK], F32) for i in range(2)]; b_res = [Buf(f"res{i}") for i in range(2)]
    rr = [P.sb(f"rr{i}", [128, NTOK], F32) for i in range(2)]; b_rr = [Buf(f"rr{i}") for i in range(2)]
    s.op("dve", lambda e: e.tensor_scalar(onesd[:, :], cst_f[:, 128:256], 128.0 / D, None, ALU.mult), [b_cst], [b_cst])
    s.op("dve", lambda e: e.memset(lneps[:, :], LN_EPS), [], [b_rstd])
    s.dma("sp", lnp[:, :, :], lnv, writes=[b_lnp])
    outer_stack = P.stack
    P.stack = ExitStack()
    yT = P.sb("yT", [128, 64, NTOK], BF16)
    b_yT = [Buf(f"yT{c}") for c in range(64)]
    if yu is None:
        for c in range(64):
            s.dma("sp", yT[:, c, :], yT_d[c], reads=[b_yTd], writes=[b_yT[c]])
    else:
        yu_d, rs_d, b_yud = yu
        stg = res + [P.sb(f"yst{i}", [128, NTOK], F32) for i in range(2)]
        b_stg = b_res + [Buf("yst0"), Buf("yst1")]
        for c in range(64):
            hd = c // 8
            i = c % 4
            if c % 8 == 0:
                s.dma("sp", rr[hd % 2][:, :], rs_d[hd], reads=[b_yud], writes=[b_rr[hd % 2]])
            s.dma("sp", stg[i][:, :], yu_d[c], reads=[b_yud], writes=[b_stg[i]])
            s.op("dve", lambda e: e.tensor_tensor(yT[:, c, :], stg[i][:, :], rr[hd % 2][:, :], ALU.mult),
                 [b_stg[i], b_rr[hd % 2]], [b_yT[c]])
    NWO = 2
    wo = [P.sb(f"wo{i}", [128, 64, 128], BF16) for i in range(NWO)]; b_wo = [Buf(f"wo{i}") for i in range(NWO)]
    rhi = [P.sb(f"rhi{i}", [128, TT], BF16) for i in range(2)]; b_rhi = [Buf(f"rhi{i}") for i in range(2)]
    rlo = [P.sb(f"rlo{i}", [128, TT], BF16) for i in range(2)]; b_rlo = [Buf(f"rlo{i}") for i in range(2)]
    rsq = [P.sb(f"rsq{i}", [128, TT], BF16) for i in range(2)]; b_rsq = [Buf(f"rsq{i}") for i in range(2)]
    b_rTd = [Buf(f"rTd{k}") for k in range(KC)]
    pc = 0
    WSO = WStream(P, [w_out[dm] for dm in range(KC)], wo, b_wo)
    pending = []

    def flush_stats():
        while pending:
            tt_, si_, first, last = pending.pop(0)
            s.op("pe", lambda e: e.matmul(ps_st[tt_][:, 0:TT], onesd[:, :], rhi[si_][:, :], start=first, stop=False),
                 [b_rhi[si_], b_cst], [b_st[tt_]])
            s.op("pe", lambda e: e.matmul(ps_st[tt_][:, 0:TT], onesd[:, :], rlo[si_][:, :], start=False, stop=last),
                 [b_rlo[si_], b_cst], [b_st[tt_]])
            s.op("pe", lambda e: e.matmul(ps_st[3 + tt_][:, 0:TT], onesd[:, :], rsq[si_][:, :], start=first, stop=last),
                 [b_rsq[si_], b_cst], [b_st[3 + tt_]])

    for dm in range(KC):
        wi = WSO.get(dm)
        ri = dm % 2
        s.dma("sp", res[ri][:, :], resT[dm], writes=[b_res[ri]])
        for tt in range(NTT):
            pb = P.pbank()
            pc += 1
            ts = slice(tt * TT, (tt + 1) * TT)
            for c in range(64):
                s.op("pe", lambda e: e.matmul(ps_acc[pb][:, 0:TT], wo[wi][:, c, :], yT[:, c, ts], start=(c == 0), stop=(c == 63)),
                     [b_wo[wi], b_yT[c]], [b_acc[pb]])
            flush_stats()
            si = pc % 2
            s.op("dve", lambda e: e.scalar_tensor_tensor(rr[ri][:, ts], res[ri][:, ts], ALPHA, ps_acc[pb][:, 0:TT], ALU.mult, ALU.add),
                 [b_res[ri], b_acc[pb]], [b_rr[ri]])
            s.op("act", lambda e: e.copy(rhi[si][:, :], rr[ri][:, ts]), [b_rr[ri]], [b_rhi[si]])
            s.op("dve", lambda e: e.tensor_tensor(rlo[si][:, :], rr[ri][:, ts], rhi[si][:, :], ALU.subtract),
                 [b_rr[ri], b_rhi[si]], [b_rlo[si]])
            s.op("act", lambda e: e.activation(rsq[si][:, :], rr[ri][:, ts], AF.Square), [b_rr[ri]], [b_rsq[si]])
            pending.append((tt, si, dm == 0, dm == KC - 1))
        s.dma("sp", rT_d[dm], rr[ri][:, :], reads=[b_rr[ri]], writes=[b_rTd[dm]])
    flush_stats()
    for tt in range(NTT):
        ts = slice(tt * TT, (tt + 1) * TT)
        s.op("act", lambda e: e.copy(mean[:, ts], ps_st[tt][:, 0:TT]), [b_st[tt]], [b_mean])
        s.op("dve", lambda e: e.tensor_tensor(rstd[:, ts], mean[:, ts], mean[:, ts], ALU.mult), [b_mean], [b_rstd])
        s.op("dve", lambda e: e.tensor_tensor(rstd[:, ts], ps_st[3 + tt][:, 0:TT], rstd[:, ts], ALU.subtract), [b_st[3 + tt], b_rstd], [b_rstd])
    s.op("act", lambda e: e.activation(rstd[:, :], rstd[:, :], AF.Ln, bias=lneps[:, 0:1], scale=1.0), [b_rstd], [b_rstd])
    s.op("act", lambda e: e.activation(rstd[:, :], rstd[:, :], AF.Exp, scale=-0.5), [b_rstd], [b_rstd])
    b_out = Buf("hT_out")
    s.barrier()
    P.stack.close()
    P.stack = outer_stack
    for i in range(2, 4):
        res.append(P.sb(f"res{i}", [128, NTOK], F32)); b_res.append(Buf(f"res{i}"))
        rr.append(P.sb(f"rr{i}", [128, NTOK], F32)); b_rr.append(Buf(f"rr{i}"))
    for kc in range(KC):
        ri = kc % 4
        s.dma("sp", rr[ri][:, :], rT_d[kc], reads=[b_rTd[kc]], writes=[b_rr[ri]])
        s.op("dve", lambda e: e.tensor_tensor(res[ri][:, :], rr[ri][:, :], mean[:, :], ALU.subtract), [b_rr[ri], b_mean], [b_res[ri]])
        s.op("dve", lambda e: e.tensor_tensor(res[ri][:, :], res[ri][:, :], rstd[:, :], ALU.mult), [b_res[ri], b_rstd], [b_res[ri]])
        s.op("act", lambda e: e.activation(res[ri][:, :], res[ri][:, :], AF.Identity, bias=lnp[:, 1, kc:kc + 1], scale=lnp[:, 0, kc:kc + 1]),
             [b_res[ri], b_lnp], [b_res[ri]])
        s.dma("pool", hT_out[kc], res[ri][:, :], reads=[b_res[ri]], writes=[b_out])
    s.drain_dma("sp")
    s.drain_dma("pool")
    return b_out


def build_fused(H0=64, H1=8, groups=None, xmode="cc", layers=(0, 1), do_out=True, debug=False):
    P = Prog()
    nc, s = P.nc, P.s
    groups = groups or [[0, 1], [2, 3], [4, 5], [6, 7]]
    cst = P.din("cst", [128, 128 + 128 + CH + NTOK])
    flag_d = P.din("flag", [128, 1])
    cst_f = P.sb("cst_f", [128, 128 + 128 + CH + NTOK], F32, persist=True)
    ident = P.sb("ident", [128, 128], BF16, persist=True)
    ones128 = P.sb("ones128", [128, 128], BF16, persist=True)
    ones1024 = P.sb("ones1024", [128, 128], BF16, persist=True)
    eps_c = P.sb("eps_c", [128, 1], F32, persist=True)
    flag = P.sb("flag_sb", [128, 1], F32, persist=True)
    b_cst, b_flag = Buf("cst"), Buf("flag")
    s.dma("sp", cst_f[:, :], cst, writes=[b_cst])
    s.dma("sp", flag[:, :], flag_d, writes=[b_flag])
    s.op("dve", lambda e: e.memset(eps_c[:, :], RMS_EPS), [], [b_cst])
    s.op("dve", lambda e: e.tensor_copy(ident[:, :], cst_f[:, 0:128]), [b_cst], [b_cst])
    s.op("dve", lambda e: e.tensor_copy(ones128[:, :], cst_f[:, 128:256]), [b_cst], [b_cst])
    s.op("dve", lambda e: e.tensor_scalar(ones1024[:, :], cst_f[:, 128:256], 128.0 / 1024, None, ALU.mult), [b_cst], [b_cst])
    C = dict(cst_f=cst_f, ident=ident, ones128=ones128, ones1024=ones1024, eps_c=eps_c, b_cst=b_cst, flag=flag, b_flag=b_flag)
    dbg = P.dout if debug else P.dscr
    xT = P.din("xT", [KC, 128, NTOK])
    b_h1 = Buf("h1T")
    if 0 in layers:
        w_in0 = P.din("w_in0", [H0, 4, 128, KC, 128])
        vecs0 = P.din("vecs0", [128, 5, 64])
        yT_d = dbg("yT_d", [64, 128, NTOK], BF16)
        if do_out:
            w_out0 = P.din("w_out0", [KC, 128, 64, 128])
            lnv0 = P.din("lnv0", [128, 2, KC])
            rT_d = P.dscr("rT_d", [KC, 128, NTOK], F32)
            h1T = (P.dout if (debug or 1 not in layers) else P.dscr)("h1T", [KC, 128, NTOK], F32)
        else:
            w_out0 = lnv0 = rT_d = h1T = None
        emit_layer0(P, C, xT, w_in0, vecs0, w_out0, lnv0, yT_d, rT_d, h1T, groups, xmode, H=H0, do_out=do_out)
        if do_out:
            P.new_phase()
        x1 = h1T
    else:
        x1 = xT
    if 1 in layers:
        w_in1 = P.din("w_in1", [H1, 24, 128, KC, 128])
        wg1 = P.din("wg1", [128, KC, 16])
        wg2 = P.din("wg2", [16, 4096])
        vecs1 = P.din("vecs1", [128, 96])
        yu_d = dbg("yu_d", [64, 128, NTOK], F32)
        rs_d = dbg("rs_d", [8, 128, NTOK], F32)
        if do_out:
            w_out1 = P.din("w_out1", [KC, 128, 64, 128])
            lnv1 = P.din("lnv1", [128, 2, KC])
            rT_d1 = P.dscr("rT_d1", [KC, 128, NTOK], F32)
            outT = P.dout("outT", [KC, 128, NTOK], F32)
        else:
            w_out1 = lnv1 = rT_d1 = outT = None
        emit_layer1(P, C, x1, b_h1, w_in1, wg1, wg2, vecs1, w_out1, lnv1, yu_d, rs_d, rT_d1, outT, groups, xmode, H=H1, do_out=do_out)
    s.barrier()
    return P


def _relayout_w_in0(w):
    return np.ascontiguousarray(w.reshape(32, 128, 4, 64, 128).transpose(3, 2, 1, 0, 4))


def _relayout_w_out(w):
    return np.ascontiguousarray(w.reshape(64, 128, 32, 128).transpose(2, 1, 0, 3))


def _relayout_w_in1(w):
    wr = w.reshape(32, 128, 24576)
    out = np.empty((8, 24, 128, 32, 128), np.float32)
    for hd in range(8):
        for dt in range(4):
            out[hd, 2 * dt] = wr[:, :, hd * 512 + dt * 128: hd * 512 + (dt + 1) * 128].transpose(1, 0, 2)
            out[hd, 2 * dt + 1] = wr[:, :, 4096 + hd * 512 + dt * 128: 4096 + hd * 512 + (dt + 1) * 128].transpose(1, 0, 2)
        for v in range(8):
            out[hd, 8 + 2 * v] = wr[:, :, 8192 + hd * 1024 + v * 128: 8192 + hd * 1024 + (v + 1) * 128].transpose(1, 0, 2)
            out[hd, 9 + 2 * v] = wr[:, :, 16384 + hd * 1024 + v * 128: 16384 + hd * 1024 + (v + 1) * 128].transpose(1, 0, 2)
    return out


def _make_cst():
    cst = np.zeros((128, 256 + CH + NTOK), np.float32)
    cst[:, 0:128] = np.eye(128, dtype=np.float32)
    cst[:, 128:256] = 1.0 / 128
    jj = np.arange(CH)
    cst[0:CH, 256:256 + CH] = (jj[:, None] <= jj[None, :]).astype(np.float32)
    sm = np.ones(NTOK, np.float32)
    sm[::CH] = 0
    cst[:, 256 + CH:] = sm[None, :]
    return cst


_PROG = {}


def kernel(x, meta, lb_logits, l0_w_in, l0_b_f, l0_norm_g, l0_w_out, l0_ln_g, l0_ln_b,
           l1_w_in, l1_w_g1, l1_w_g2, l1_b_g, l1_norm_g, l1_w_out, l1_ln_g, l1_ln_b):
    f = lambda a: np.ascontiguousarray(np.asarray(a, dtype=np.float32))
    x, meta = f(x), f(meta)
    B = x.shape[0]
    h0 = np.concatenate([np.broadcast_to(meta[None], (B, 16, D)), x], axis=1)
    lbl = f(lb_logits)
    vecs0 = np.stack([f(l0_b_f), f(l0_norm_g), lbl[0], lbl[1], lbl[2]], 0)
    shared = {
        "cst": _make_cst(),
        "w_in0": _relayout_w_in0(f(l0_w_in)),
        "vecs0": np.ascontiguousarray(vecs0.reshape(5, 64, 128).transpose(2, 0, 1)),
        "w_out0": _relayout_w_out(f(l0_w_out)),
        "lnv0": np.ascontiguousarray(np.stack([f(l0_ln_g), f(l0_ln_b)], 0).reshape(2, 32, 128).transpose(2, 0, 1)),
        "w_in1": _relayout_w_in1(f(l1_w_in)),
        "wg1": np.ascontiguousarray(f(l1_w_g1).reshape(32, 128, 16).transpose(1, 0, 2)),
        "wg2": f(l1_w_g2),
        "vecs1": np.ascontiguousarray(np.concatenate([f(l1_norm_g).reshape(64, 128).T, f(l1_b_g).reshape(32, 128).T], 1)),
        "w_out1": _relayout_w_out(f(l1_w_out)),
        "lnv1": np.ascontiguousarray(np.stack([f(l1_ln_g), f(l1_ln_b)], 0).reshape(2, 32, 128).transpose(2, 0, 1)),
    }
    in_maps = []
    for c in range(NCORES):
        b, half = c // 2, c % 2
        d = dict(shared)
        d["xT"] = np.ascontiguousarray(h0[b, half * NTOK:(half + 1) * NTOK].T).reshape(KC, 128, NTOK)
        d["flag"] = np.full((128, 1), float(half), np.float32)
        in_maps.append(d)
    if "p" not in _PROG:
        _PROG["p"] = build_fused()
    res = run_bass_kernel_spmd(_PROG["p"].nc, in_maps, core_ids=list(range(NCORES)))
    out = np.empty((B, 2064, D), np.float32)
    for c in range(NCORES):
        b, half = c // 2, c % 2
        out[b, half * NTOK:(half + 1) * NTOK] = res.results[c]["outT"].reshape(D, NTOK).T
    return np.ascontiguousarray(out[:, 16:])
```

```python
import numpy as np
import concourse.bass as bass
import concourse.mybir as mybir
from concourse.bass_utils import run_bass_kernel_spmd
from contextlib import ExitStack

F32 = mybir.dt.float32
BF16 = mybir.dt.bfloat16
AF = mybir.ActivationFunctionType
ALU = mybir.AluOpType

D = 4096
DI = 8192
NTOK = 1032
CH = 86
NCH = 12
TT = 344
NTT = 3
CPT = 4
KC = 32
NCORES = 8
ALPHA = (2.0 * 2) ** 0.25
LN_EPS = 1e-5
RMS_EPS = 1e-6


class Buf:
    __slots__ = ("name", "w", "r")

    def __init__(self, name=""):
        self.name = name
        self.w = None
        self.r = []


class Sched:
    ENG = ("pe", "act", "dve", "pool", "sp")

    def __init__(self, nc, n_dma_sems=16):
        self.nc = nc
        self.eng = {"pe": nc.tensor, "act": nc.scalar, "dve": nc.vector,
                    "pool": nc.gpsimd, "sp": nc.sync}
        self.sems = {}
        self.cnt = {}
        self.semobj = {}
        for e in self.ENG:
            self.sems[e] = nc.alloc_semaphore(name=f"s_{e}")
            self.cnt[e] = 0
            self.semobj[("c", e)] = self.sems[e]
        self.dma_sems = {}
        self.dma_cnt = {}
        self.dma_rr = {}
        for q in ("sp", "pool"):
            self.dma_sems[q] = [nc.alloc_semaphore(name=f"d_{q}{i}") for i in range(n_dma_sems)]
            self.dma_cnt[q] = [0] * n_dma_sems
            self.dma_rr[q] = 0
            for i, sm in enumerate(self.dma_sems[q]):
                self.semobj[("d", q, i)] = sm
        self.waited = {e: {} for e in self.ENG}
        self.n_wait = 0

    def _wait(self, e, key, val):
        if key == ("c", "pe") and e == "pe":
            return
        w = self.waited[e]
        if w.get(key, 0) >= val:
            return
        w[key] = val
        self.eng[e].wait_ge(self.semobj[key], val)
        self.n_wait += 1

    def _deps(self, e, reads, writes):
        for b in reads:
            if b.w is not None:
                self._wait(e, *b.w)
        for b in writes:
            if b.w is not None:
                self._wait(e, *b.w)
            for ev in b.r:
                self._wait(e, *ev)

    def _post(self, ev, reads, writes):
        for b in writes:
            b.w = ev
            b.r = []
        for b in reads:
            if b in writes:
                continue
            b.r.append(ev)
            if len(b.r) > 8:
                best = {}
                for k, v in b.r:
                    if best.get(k, 0) < v:
                        best[k] = v
                b.r = list(best.items())

    def op(self, e, fn, reads=(), writes=()):
        self._deps(e, reads, writes)
        ins = fn(self.eng[e])
        self.cnt[e] += 1
        ins.then_inc(self.sems[e], 1)
        ev = (("c", e), self.cnt[e])
        self._post(ev, reads, writes)
        return ev

    def dma(self, q, out, in_, reads=(), writes=()):
        self._deps(q, reads, writes)
        i = self.dma_rr[q]
        self.dma_rr[q] = (i + 1) % len(self.dma_sems[q])
        key = ("d", q, i)
        if self.dma_cnt[q][i] > 0:
            self._wait(q, key, self.dma_cnt[q][i])
        self.dma_cnt[q][i] += 16
        ins = self.eng[q].dma_start(out=out, in_=in_)
        ins.then_inc(self.dma_sems[q][i], 16)
        ev = (key, self.dma_cnt[q][i])
        self._post(ev, reads, writes)
        return ev

    def drain_dma(self, q, e=None):
        e = e or q
        for i, c in enumerate(self.dma_cnt[q]):
            if c:
                self._wait(e, ("d", q, i), c)

    def barrier(self):
        for e in self.ENG:
            for e2 in self.ENG:
                if e2 != e and self.cnt[e2]:
                    self._wait(e, ("c", e2), self.cnt[e2])
            for q in self.dma_sems:
                self.drain_dma(q, e)


def interleave(ga, gb, ratio):
    da = db = False
    acc = 0.0
    while not (da and db):
        acc += ratio
        while acc >= 1.0 and not da:
            acc -= 1.0
            try:
                next(ga)
            except StopIteration:
                da = True
        if not db:
            try:
                next(gb)
            except StopIteration:
                db = True
        if db and not da:
            for _ in ga:
                pass
            da = True


class Prog:
    def __init__(self):
        self.nc = bass.Bass("TRN2", target_bir_lowering=False)
        self.s = Sched(self.nc)
        self.stack = ExitStack()
        nc = self.nc
        mk = lambda n: [nc.alloc_psum_tensor(f"{n}{i}", [128, 512], F32) for i in range(2)]
        self.ps_proj, self.ps_m, self.ps_o, self.ps_sd = mk("ps_proj"), mk("ps_m"), mk("ps_o"), mk("ps_sd")
        self.b_proj = [Buf("pj0"), Buf("pj1")]
        self.b_m = [Buf("m0"), Buf("m1")]
        self.b_o = [Buf("o0"), Buf("o1")]
        self.b_sd = [Buf("sd0"), Buf("sd1")]
        self.mcount = 0
        self.pcount = 0

    def mbank(self):
        i = self.mcount % 2
        self.mcount += 1
        return i

    def pbank(self):
        i = self.pcount % 2
        self.pcount += 1
        return i

    def sb(self, name, shape, dt, persist=False):
        self.nalloc = getattr(self, "nalloc", 0) + 1
        name = f"{name}_{self.nalloc}"
        if persist:
            return self.nc.alloc_sbuf_tensor(name, list(shape), dt)
        return self.stack.enter_context(self.nc.sbuf_tensor(name, list(shape), dt))

    def new_phase(self):
        self.s.barrier()
        self.stack.close()
        self.stack = ExitStack()

    def din(self, name, shape, dt=F32):
        return self.nc.dram_tensor(name, list(shape), dt, kind="ExternalInput").ap()

    def dout(self, name, shape, dt=F32):
        return self.nc.dram_tensor(name, list(shape), dt, kind="ExternalOutput").ap()

    def dscr(self, name, shape, dt=F32):
        return self.nc.dram_tensor(name, list(shape), dt, kind="Internal").ap()


class WStream:
    def __init__(self, P, tiles, bufs, bbufs):
        self.P, self.tiles, self.bufs, self.bb = P, tiles, bufs, bbufs
        self.nw = len(bufs) if bufs else 0
        self.issued = 0

    def get(self, k):
        while self.issued < min(len(self.tiles), k + self.nw):
            i = self.issued % self.nw
            self.P.s.dma("pool", self.bufs[i][:, :, :], self.tiles[self.issued], writes=[self.bb[i]])
            self.issued += 1
        return k % self.nw


class Exchanger:
    def __init__(self, P, name, F, groups, xmode, nslots=4):
        self.P, self.F, self.groups, self.xmode, self.n = P, F, groups, xmode, nslots
        self.xin = [P.dscr(f"{name}_xin{k}", [128, F]) for k in range(nslots)]
        self.xout = [P.dscr(f"{name}_xout{k}", [256, F]) for k in range(nslots)]
        self.b_in = [Buf(f"xin{k}") for k in range(nslots)]
        self.b_out = [Buf(f"xout{k}") for k in range(nslots)]
        self.cnt = 0

    def send(self, S_f, b_Sf, Sp_f, b_Spf):
        s = self.P.s
        k = self.cnt % self.n
        self.cnt += 1
        s.dma("sp", self.xin[k], S_f, reads=(b_Sf if isinstance(b_Sf, list) else [b_Sf]), writes=[self.b_in[k]])
        if self.xmode == "cc":
            s.op("pool", lambda e: e.collective_compute("AllGather", ALU.bypass, replica_groups=self.groups,
                                                        ins=[self.xin[k]], outs=[self.xout[k]]),
                 [self.b_in[k]], [self.b_out[k]])
            s.dma("sp", Sp_f, self.xout[k][0:128, :], reads=[self.b_out[k]], writes=[b_Spf])
        else:
            s.dma("sp", Sp_f, self.xin[k], reads=[self.b_in[k]], writes=[b_Spf])

    def recv(self, Sp_f, b_Spf, Sp_b, b_Spb, flag, b_flag):
        self.P.s.op("dve", lambda e: e.tensor_scalar(Sp_b, Sp_f, flag, None, ALU.mult), [b_Spf, b_flag], [b_Spb])


def emit_layer0(P, C, xT, w_in, vecs, w_out, lnv, yT_d, rT_d, hT_out, groups, xmode, H=64, do_out=True):
    nc, s = P.nc, P.s
    ps_proj, b_proj, ps_m, b_m, ps_o, b_o, ps_sd, b_sd = P.ps_proj, P.b_proj, P.ps_m, P.b_m, P.ps_o, P.b_o, P.ps_sd, P.b_sd
    cst_f, ident, ones, eps_c, b_cst, flag, b_flag = C["cst_f"], C["ident"], C["ones128"], C["eps_c"], C["b_cst"], C["flag"], C["b_flag"]
    cmask = cst_f[0:CH, 256:256 + CH]
    smask = cst_f[:, 256 + CH:256 + CH + NTOK]
    mbank = P.mbank

    vec = P.sb("vec", [128, 5, 64], F32)
    lb = P.sb("lb", [128, 64], F32)
    oml = P.sb("oml", [128, 64], F32)
    noml = P.sb("noml", [128, 64], F32)
    lbt = P.sb("lbt", [128, 3, 64], F32)
    b_vec = Buf("vec")
    s.dma("sp", vec[:, :, :], vecs, writes=[b_vec])
    s.op("act", lambda e: e.activation(lbt[:, :, :], vec[:, 2:5, :], AF.Exp), [b_vec], [b_vec])
    s.op("dve", lambda e: e.tensor_add(lb[:, :], lbt[:, 0, :], lbt[:, 1, :]), [b_vec], [b_vec])
    s.op("dve", lambda e: e.tensor_add(lb[:, :], lb[:, :], lbt[:, 2, :]), [b_vec], [b_vec])
    s.op("dve", lambda e: e.reciprocal(lb[:, :], lb[:, :]), [b_vec], [b_vec])
    s.op("dve", lambda e: e.tensor_mul(lb[:, :], lb[:, :], lbt[:, 0, :]), [b_vec], [b_vec])
    s.op("dve", lambda e: e.tensor_scalar(oml[:, :], lb[:, :], -1.0, 1.0, ALU.mult, ALU.add), [b_vec], [b_vec])
    s.op("dve", lambda e: e.tensor_scalar(noml[:, :], lb[:, :], 1.0, -1.0, ALU.mult, ALU.add), [b_vec], [b_vec])

    hT = P.sb("hT", [128, KC, NTOK], BF16)
    b_hTg = [Buf(f"hT{g}") for g in range(8)]
    NW = 4
    wt = [P.sb(f"wt{i}", [128, KC, 128], BF16) for i in range(NW)]
    b_wt = [Buf(f"wt{i}") for i in range(NW)]
    WS = WStream(P, [w_in[h, p] for h in range(H) for p in range(4)], wt, b_wt)
    for g in range(8):
        s.dma("pool", hT[:, g * 4:(g + 1) * 4, :], xT[g * 4:(g + 1) * 4].rearrange("k p t -> p k t"), writes=[b_hTg[g]])
        if g == 0:
            WS.get(0)
    q_f = P.sb("q_f", [128, NTOK], F32); b_qf = Buf("q_f")
    sg = P.sb("sg", [128, NTOK], F32); b_sg = Buf("sg")
    k_f = P.sb("k_f", [128, NTOK], F32); b_kf = Buf("k_f")
    lg = P.sb("lg", [128, NTOK], F32); b_lg = Buf("lg")
    bb = P.sb("bb", [128, NTOK], F32); b_bb = Buf("bb")
    e2 = P.sb("e2", [128, NTOK], F32); b_e2 = Buf("e2")
    vT = P.sb("vT", [128, NTOK], BF16); b_vT = Buf("vT")
    kdT = P.sb("kdT", [128, NTOK], BF16); b_kdT = Buf("kdT")
    pt = P.sb("pt", [128, NCH + 1], F32); b_pt = Buf("pt")
    zer = P.sb("zer", [128, NCH], F32)
    s.op("dve", lambda e: e.memset(pt[:, :], 1.0), [], [b_pt])
    s.op("dve", lambda e: e.memset(zer[:, :], 0.0), [], [b_pt])
    e1 = [P.sb(f"e1_{i}", [128, NTOK], F32) for i in range(2)]; b_e1 = [Buf("e1_0"), Buf("e1_1")]
    qd = [P.sb(f"qd{i}", [128, NTOK], BF16) for i in range(2)]; b_qd = [Buf("qd0"), Buf("qd1")]
    qg = [P.sb(f"qg{i}", [128, NTOK], BF16) for i in range(3)]; b_qg = [Buf(f"qg{i}") for i in range(3)]
    ki = [P.sb(f"ki{i}", [128, NTOK], BF16) for i in range(2)]; b_ki = [Buf("ki0"), Buf("ki1")]
    kdt = [P.sb(f"kdt{i}", [128, NCH, 128], BF16) for i in range(2)]; b_kdt = [Buf("kdt0"), Buf("kdt1")]
    vt = [P.sb(f"vt{i}", [128, NCH, 128], BF16) for i in range(2)]; b_vt = [Buf("vt0"), Buf("vt1")]
    a_sb = [P.sb(f"a_sb{i}", [128, NCH, CH], BF16) for i in range(2)]; b_asb = [Buf("a0"), Buf("a1")]
    sz = [P.sb(f"sz{i}", [128, NTOK], F32) for i in range(3)]; b_sz = [Buf("sz0"), Buf("sz1"), Buf("sz2")]
    S_f = P.sb("S_f", [128, 128], F32); b_Sf = Buf("S_f")
    S_b = [P.sb(f"S_b{i}", [128, 128], BF16) for i in range(2)]; b_Sb = [Buf("S_b0"), Buf("S_b1")]
    Sp_f = [P.sb(f"Sp_f{i}", [128, 128], F32) for i in range(2)]; b_Spf = [Buf("spf0"), Buf("spf1")]
    Sp_b = [P.sb(f"Sp_b{i}", [128, 128], BF16) for i in range(2)]; b_Spb = [Buf("spb0"), Buf("spb1")]
    o_f = [P.sb(f"o_f{i}", [128, NTOK], F32) for i in range(2)]; b_of = [Buf("of0"), Buf("of1")]
    sq = [P.sb(f"sq{i}", [128, TT], BF16) for i in range(3)]; b_sq = [Buf("sq0"), Buf("sq1"), Buf("sq2")]
    rstd = [P.sb(f"rstd{i}", [128, TT], F32) for i in range(2)]; b_rstd = [Buf("rs0"), Buf("rs1")]
    t1 = [P.sb(f"t1_{i}", [128, TT], F32) for i in range(2)]; b_t1 = [Buf("t10"), Buf("t11")]
    y_b = [P.sb(f"y_b{i}", [128, NTOK], BF16) for i in range(2)]; b_yb = [Buf("yb0"), Buf("yb1")]
    b_yTd = Buf("yT_d")
    X = Exchanger(P, "x0", 128, groups, xmode)
    wcount = [0]
    tsl = lambda tt: slice(tt * TT, (tt + 1) * TT)

    def proj(wi, col, evac):
        for tt in range(NTT):
            pb = P.pbank()
            for kc in range(KC):
                s.op("pe", lambda e: e.matmul(ps_proj[pb][:, 0:TT], wt[wi][:, kc, :],
                                              hT[:, kc, tsl(tt)], start=(kc == 0), stop=(kc == KC - 1)),
                     [b_wt[wi], b_hTg[kc // 4]], [b_proj[pb]])
                if kc % 8 == 7:
                    yield
            evac(tt, ps_proj[pb][:, 0:TT], b_proj[pb])

    def stage1(h):
        par = h % 2
        hc = slice(h, h + 1)
        def ev_q(tt, ps, bps):
            s.op("act", lambda e: e.activation(q_f[:, tsl(tt)], ps, AF.Silu), [bps], [b_qf])
        yield from proj(WS.get(4 * h), 0, ev_q)

        def ev_f(tt, ps, bps):
            s.op("act", lambda e: e.activation(sg[:, tsl(tt)], ps, AF.Sigmoid, bias=vec[:, 0, hc], scale=1.0),
                 [bps, b_vec], [b_sg])
        yield from proj(WS.get(4 * h + 1), 128, ev_f)
        def ev_i(tt, ps, bps):
            s.op("act", lambda e: e.copy(vT[:, tsl(tt)], ps), [bps], [b_vT])
        gi = proj(WS.get(4 * h + 2), 0, ev_i)

        def chain():
            s.op("dve", lambda e: e.tensor_scalar(k_f[:, :], sg[:, :], noml[:, hc], oml[:, hc], ALU.mult, ALU.add),
                 [b_sg, b_vec], [b_kf])
            s.op("act", lambda e: e.activation(lg[:, :], sg[:, :], AF.Ln, bias=lb[:, hc], scale=oml[:, hc]),
                 [b_sg, b_vec], [b_lg])
            yield
            s.op("dve", lambda e: e.tensor_tensor_scan(bb[:, :], smask, lg[:, :], 0.0, ALU.mult, ALU.add),
                 [b_lg, b_cst], [b_bb])
            yield
            s.op("act", lambda e: e.activation(e1[par][:, :], bb[:, :], AF.Exp), [b_bb], [b_e1[par]])
            s.op("act", lambda e: e.activation(e2[:, :], bb[:, :], AF.Exp, scale=-1.0), [b_bb], [b_e2])
            yield
            e1c = e1[par][:, :].rearrange("p (c j) -> p c j", j=CH)
            s.op("dve", lambda e: e.tensor_copy(lg[:, 0:NCH], e1c[:, :, CH - 1]), [b_e1[par], b_lg], [b_lg])
            s.op("dve", lambda e: e.tensor_tensor_scan(pt[:, 1:NCH + 1], lg[:, 0:NCH], zer[:, :], 1.0, ALU.mult, ALU.add),
                 [b_lg, b_pt], [b_pt])
            s.op("dve", lambda e: e.tensor_tensor(q_f[:, :], q_f[:, :], e1[par][:, :], ALU.mult), [b_qf, b_e1[par]], [b_qf])
            yield
            s.op("act", lambda e: e.copy(qd[par][:, :], q_f[:, :]), [b_qf], [b_qd[par]])
            g3 = h % 3
            s.op("dve", lambda e: e.tensor_tensor(qg[g3][:, :].rearrange("p (c j) -> p c j", j=CH),
                                                  q_f[:, :].rearrange("p (c j) -> p c j", j=CH),
                                                  pt[:, 0:NCH].unsqueeze(2).to_broadcast([128, NCH, CH]), ALU.mult),
                 [b_qf, b_pt], [b_qg[g3]])
            yield
            s.op("dve", lambda e: e.tensor_tensor(e2[:, :], k_f[:, :], e2[:, :], ALU.mult), [b_kf, b_e2], [b_e2])
            yield
            s.op("act", lambda e: e.copy(ki[par][:, :], e2[:, :]), [b_e2], [b_ki[par]])
            e1l = e1c[:, :, CH - 1:CH].to_broadcast([128, NCH, CH])
            s.op("dve", lambda e: e.tensor_tensor(kdT[:, :].rearrange("p (c j) -> p c j", j=CH),
                                                  e2[:, :].rearrange("p (c j) -> p c j", j=CH), e1l, ALU.mult),
                 [b_e2, b_e1[par]], [b_kdT])
            yield


        for _ in chain():
            yield
            for _k in range(2):
                if next(gi, "done") != "done":
                    yield
        for _ in gi:
            yield

        def ev_z(tt, ps, bps):
            s.op("act", lambda e: e.activation(sz[h % 3][:, tsl(tt)], ps, AF.Silu), [bps], [b_sz[h % 3]])
        yield from proj(WS.get(4 * h + 3), 128, ev_z)
        for g in range(3):
            mb = mbank()
            for j in range(4):
                c = g * 4 + j
                cs = slice(c * CH, (c + 1) * CH)
                s.op("pe", lambda e: e.matmul(ps_m[mb][0:CH, j * 128:j * 128 + CH], ki[par][:, cs], qd[par][:, cs],
                                              start=True, stop=True), [b_ki[par], b_qd[par]], [b_m[mb]])
            s.op("dve", lambda e: e.tensor_tensor(a_sb[par][0:CH, g * 4:(g + 1) * 4, :],
                                                  ps_m[mb][0:CH, :].rearrange("p (c v) -> p c v", v=128)[:, :, 0:CH],
                                                  cmask.unsqueeze(1).to_broadcast([CH, 4, CH]), ALU.mult),
                 [b_m[mb], b_cst], [b_asb[par]])
        yield
        for g in range(3):
            mb = mbank()
            pv = ps_m[mb][:, :].bitcast(BF16)
            for j in range(4):
                c = g * 4 + j
                s.op("pe", lambda e: e.transpose(pv[0:CH, j * 128:(j + 1) * 128], kdT[:, c * CH:(c + 1) * CH], ident[:, :]),
                     [b_kdT, b_cst], [b_m[mb]])
            s.op("act", lambda e: e.copy(kdt[par][0:CH, g * 4:(g + 1) * 4, :],
                                         pv[0:CH, 0:512].rearrange("p (c v) -> p c v", v=128)),
                 [b_m[mb]], [b_kdt[par]])
        yield
        for g in range(3):
            mb = mbank()
            pv = ps_m[mb][:, :].bitcast(BF16)
            for j in range(4):
                c = g * 4 + j
                s.op("pe", lambda e: e.transpose(pv[0:CH, j * 128:(j + 1) * 128], vT[:, c * CH:(c + 1) * CH], ident[:, :]),
                     [b_vT, b_cst], [b_m[mb]])
            s.op("dve", lambda e: e.tensor_copy(vt[par][0:CH, g * 4:(g + 1) * 4, :],
                                                pv[0:CH, 0:512].rearrange("p (c v) -> p c v", v=128)),
                 [b_m[mb]], [b_vt[par]])
        yield

    def finalize(h):
        par = h % 2
        hc = slice(h, h + 1)
        g3 = h % 3
        X.recv(Sp_f[par][:, :], b_Spf[par], Sp_b[par][:, :], b_Spb[par], flag[:, 0:1], b_flag)
        for tt in range(NTT):
            mb = mbank()
            s.op("pe", lambda e: e.matmul(ps_m[mb][:, 0:TT], Sp_b[par][:, :], qg[g3][:, tsl(tt)], start=True, stop=True),
                 [b_Spb[par], b_qg[g3]], [b_m[mb]])
            s.op("dve", lambda e: e.tensor_tensor(o_f[par][:, tsl(tt)], o_f[par][:, tsl(tt)], ps_m[mb][:, 0:TT], ALU.add),
                 [b_of[par], b_m[mb]], [b_of[par]])
            s.op("act", lambda e: e.activation(sq[tt][:, :], o_f[par][:, tsl(tt)], AF.Square), [b_of[par]], [b_sq[tt]])
            yield
        for tt in range(NTT):
            ob = tt % 2
            mb = mbank()
            s.op("pe", lambda e: e.matmul(ps_m[mb][:, 0:TT], ones[:, :], sq[tt][:, :], start=True, stop=True),
                 [b_sq[tt], b_cst], [b_m[mb]])
            s.op("act", lambda e: e.activation(rstd[ob][:, :], ps_m[mb][:, 0:TT], AF.Ln, bias=eps_c[:, 0:1], scale=1.0),
                 [b_m[mb], b_cst], [b_rstd[ob]])
            s.op("act", lambda e: e.activation(rstd[ob][:, :], rstd[ob][:, :], AF.Exp, scale=-0.5),
                 [b_rstd[ob]], [b_rstd[ob]])
            s.op("dve", lambda e: e.scalar_tensor_tensor(t1[ob][:, :], o_f[par][:, tsl(tt)], vec[:, 1, hc], rstd[ob][:, :], ALU.mult, ALU.mult),
                 [b_of[par], b_rstd[ob], b_vec], [b_t1[ob]])
            s.op("dve", lambda e: e.tensor_tensor(y_b[par][:, tsl(tt)], t1[ob][:, :], sz[h % 3][:, tsl(tt)], ALU.mult),
                 [b_t1[ob], b_sz[h % 3]], [b_yb[par]])
            yield
        s.dma("sp", yT_d[h], y_b[par][:, :], reads=[b_yb[par]], writes=[b_yTd])

    def stage2(h):
        par = h % 2
        s.op("dve", lambda e: e.memset(S_f[:, :], 0.0), [], [b_Sf])
        s.op("dve", lambda e: e.memset(S_b[0][:, :], 0.0), [], [b_Sb[0]])
        for tt in range(NTT):
            ob = tt % 2
            for j in range(CPT):
                c = tt * CPT + j
                s.op("pe", lambda e: e.matmul(ps_sd[ob][:, j * 128:(j + 1) * 128], kdt[par][0:CH, c, :], vt[par][0:CH, c, :],
                                              start=True, stop=True), [b_kdt[par], b_vt[par]], [b_sd[ob]])
            for j in range(CPT):
                c = tt * CPT + j
                cs = slice(c * CH, (c + 1) * CH)
                osl = ps_o[ob][:, j * CH:(j + 1) * CH]
                sbi = c % 2
                s.op("pe", lambda e: e.matmul(osl, vt[par][0:CH, c, :], a_sb[par][0:CH, c, :], start=True, stop=False),
                     [b_vt[par], b_asb[par]], [b_o[ob]])
                s.op("pe", lambda e: e.matmul(osl, S_b[sbi][:, :], qd[par][:, cs], start=False, stop=True),
                     [b_Sb[sbi], b_qd[par]], [b_o[ob]])
                gl = e1[par][:, c * CH + CH - 1:c * CH + CH]
                sd = ps_sd[ob][:, j * 128:(j + 1) * 128]
                if c < NCH - 1:
                    s.op("dve", lambda e: e.scalar_tensor_tensor(S_b[1 - sbi][:, :], S_f[:, :], gl, sd, ALU.mult, ALU.add),
                         [b_Sf, b_sd[ob], b_e1[par]], [b_Sb[1 - sbi]])
                s.op("dve", lambda e: e.scalar_tensor_tensor(S_f[:, :], S_f[:, :], gl, sd, ALU.mult, ALU.add),
                     [b_Sf, b_sd[ob], b_e1[par]], [b_Sf])
                yield
            s.op("act", lambda e: e.copy(o_f[par][:, tsl(tt)], ps_o[ob][:, 0:TT]), [b_o[ob]], [b_of[par]])
            if tt == 1 and h > 0:
                yield from finalize(h - 1)
            yield
        X.send(S_f[:, :], b_Sf, Sp_f[par][:, :], b_Spf[par])

    for _ in stage1(0):
        pass
    for h in range(H):
        if h + 1 < H:
            interleave(stage1(h + 1), stage2(h), 2.7)
        else:
            for _ in stage2(h):
                pass
    for _ in finalize(H - 1):
        pass
    if not do_out:
        return
    P.new_phase()
    out_ln_phase(P, C, yT_d, w_out, lnv, xT, rT_d, hT_out, b_yTd)


def emit_layer1(P, C, xT, b_xT, w_in, wg1, wg2, vecs, w_out, lnv, yu_d, rs_d, rT_d, hT_out, groups, xmode, H=8, do_out=True):
    nc, s = P.nc, P.s
    ps_proj, b_proj, ps_m, b_m, ps_o, b_o, ps_sd, b_sd = P.ps_proj, P.b_proj, P.ps_m, P.b_m, P.ps_o, P.b_o, P.ps_sd, P.b_sd
    cst_f, ident, ones, eps_c, b_cst, flag, b_flag = C["cst_f"], C["ident"], C["ones1024"], C["eps_c"], C["b_cst"], C["flag"], C["b_flag"]
    cmask = cst_f[0:CH, 256:256 + CH]
    smask = cst_f[:, 256 + CH:256 + CH + NTOK]
    mbank = P.mbank
    ND, NV = 4, 8
    QSCALE = 512.0 ** -0.5

    vec = P.sb("vec1", [128, 96], F32)
    nbg = P.sb("nbg", [128, 32], F32)
    b_vec = Buf("vec1")
    s.dma("sp", vec[:, :], vecs, writes=[b_vec])
    s.op("dve", lambda e: e.tensor_scalar(nbg[:, :], vec[:, 64:96], -1.0, None, ALU.mult), [b_vec], [b_vec])

    hT = P.sb("hT1", [128, KC, NTOK], BF16)
    b_hTg = [Buf(f"hT1_{g}") for g in range(8)]
    for g in range(8):
        s.dma("pool", hT[:, g * 4:(g + 1) * 4, :], xT[g * 4:(g + 1) * 4].rearrange("k p t -> p k t"), reads=[b_xT], writes=[b_hTg[g]])
    wg1b = P.sb("wg1b", [128, KC, 16], BF16)
    wg2h = [P.sb(f"wg2h{i}", [16, 512], BF16) for i in range(2)]
    b_wg2 = [Buf("wg2h0"), Buf("wg2h1")]
    u_b = P.sb("u_b", [16, NTOK], BF16)
    b_g = Buf("gatew")
    s.dma("pool", wg1b[:, :, :], wg1, writes=[b_g])

    NW = 3
    wt = [P.sb(f"wu{i}", [128, KC, 128], BF16) for i in range(NW)]
    b_wt = [Buf(f"wu{i}") for i in range(NW)]
    sp_t = P.sb("sp_t", [128, NTOK], F32); b_sp = Buf("sp")
    bs_t = P.sb("bs_t", [128, NTOK], F32); b_bs = Buf("bs")
    e1 = P.sb("e1", [128, NTOK], F32); b_e1 = Buf("e1")
    e2 = P.sb("e2g", [128, NTOK], F32); b_e2 = Buf("e2")
    qd = P.sb("qd", [128, ND, NTOK], BF16); b_qd = Buf("qd")
    qg = P.sb("qg", [128, ND, NTOK], BF16); b_qg = Buf("qg")
    ki = P.sb("ki", [128, ND, NTOK], BF16); b_ki = Buf("ki")
    kdT = P.sb("kdT1", [128, NTOK], BF16); b_kdT = Buf("kdT")
    kdt = P.sb("kdt", [128, NCH, ND * 128], BF16); b_kdt = Buf("kdt")
    a_sb = P.sb("a_sb", [128, NCH, CH], BF16); b_asb = Buf("a_sb")
    glast = P.sb("glast", [128, ND, NCH], F32); b_gl = Buf("glast")
    pt = P.sb("pt1", [128, NCH + 1], F32); b_pt = Buf("pt1")
    zer = P.sb("zer1", [128, NCH], F32)
    s.op("dve", lambda e: e.memset(pt[:, :], 1.0), [], [b_pt])
    s.op("dve", lambda e: e.memset(zer[:, :], 0.0), [], [b_pt])
    vT = P.sb("vT1", [128, NTOK], BF16); b_vT = Buf("vT")
    vt_ = [P.sb(f"vtok{i}", [128, NCH, 128], BF16) for i in range(2)]; b_vt = [Buf("vt0"), Buf("vt1")]
    sz = [P.sb(f"sz1_{i}", [128, NTOK], F32) for i in range(3)]; b_sz = [Buf("sz0"), Buf("sz1"), Buf("sz2")]
    S_f = P.sb("S_f1", [128, ND * 128], F32); b_Sf = [Buf(f"S_f{d}") for d in range(ND)]
    S_b = [P.sb(f"S_b1_{i}", [128, ND * 128], BF16) for i in range(2)]
    b_Sb = [[Buf(f"S_b{i}_{d}") for d in range(ND)] for i in range(2)]
    Sp_f = [P.sb(f"Sp_f1_{i}", [128, ND * 128], F32) for i in range(2)]; b_Spf = [Buf("spf0"), Buf("spf1")]
    Sp_b = [P.sb(f"Sp_b1_{i}", [128, ND * 128], BF16) for i in range(2)]; b_Spb = [Buf("spb0"), Buf("spb1")]
    o_f = [P.sb(f"o_f1_{i}", [128, NTOK], F32) for i in range(2)]; b_of = [Buf("of0"), Buf("of1")]
    sq = [P.sb(f"sq1_{i}", [128, TT], BF16) for i in range(3)]; b_sq = [Buf("sq0"), Buf("sq1"), Buf("sq2")]
    yu_t = [P.sb(f"yu{i}", [128, TT], F32) for i in range(2)]; b_yu = [Buf("yu0"), Buf("yu1")]
    ssacc = P.sb("ssacc", [128, NTOK], F32); b_ss = Buf("ssacc")
    b_yud = Buf("yu_d")
    X = Exchanger(P, "x1", ND * 128, groups, xmode)
    tsl = lambda tt: slice(tt * TT, (tt + 1) * TT)
    dsl = lambda dt: slice(dt * 128, (dt + 1) * 128)

    for tt in range(NTT):
        pb = P.pbank()
        for kc in range(KC):
            s.op("pe", lambda e: e.matmul(ps_proj[pb][0:16, 0:TT], wg1b[:, kc, :], hT[:, kc, tsl(tt)],
                                          start=(kc == 0), stop=(kc == KC - 1)), [b_g, b_hTg[kc // 4]], [b_proj[pb]])
        s.op("act", lambda e: e.copy(u_b[0:16, tsl(tt)], ps_proj[pb][0:16, 0:TT]), [b_proj[pb]], [b_g])

    WS = WStream(P, [w_in[hd, ti] for hd in range(H) for ti in range(24)], wt, b_wt)

    def load_w(hd, ti):
        return WS.get(hd * 24 + ti)

    def proj(wi, evac):
        for tt in range(NTT):
            pb = P.pbank()
            for kc in range(KC):
                s.op("pe", lambda e: e.matmul(ps_proj[pb][:, 0:TT], wt[wi][:, kc, :], hT[:, kc, tsl(tt)],
                                              start=(kc == 0), stop=(kc == KC - 1)), [b_wt[wi], b_hTg[kc // 4]], [b_proj[pb]])
                if kc % 8 == 7:
                    yield
            evac(tt, ps_proj[pb][:, 0:TT], b_proj[pb])

    def stageA(hd, inject=None):
        hp = hd % 2
        pending_tr = []
        if hd == 0:
            s.dma("pool", wg2h[0][:, :], wg2[:, 0:512], writes=[b_wg2[0]])
        if hd + 1 < H:
            s.dma("pool", wg2h[1 - hp][:, :], wg2[:, (hd + 1) * 512:(hd + 2) * 512], writes=[b_wg2[1 - hp]])
        for dt in range(ND):
            gdt = hd * ND + dt
            wq = load_w(hd, 2 * dt)
            for tt in range(NTT):
                pb = P.pbank()
                s.op("pe", lambda e: e.matmul(ps_proj[pb][:, 0:TT], wg2h[hp][0:16, dsl(dt)], u_b[0:16, tsl(tt)],
                                              start=True, stop=True), [b_g, b_wg2[hp]], [b_proj[pb]])
                s.op("act", lambda e: e.activation(sp_t[:, tsl(tt)], ps_proj[pb][:, 0:TT], AF.Exp, bias=nbg[:, gdt:gdt + 1], scale=-1.0),
                     [b_proj[pb], b_vec], [b_sp])
            s.op("act", lambda e: e.activation(sp_t[:, :], sp_t[:, :], AF.Ln, bias=1.0, scale=1.0), [b_sp], [b_sp])
            s.op("dve", lambda e: e.tensor_tensor_scan(bs_t[:, :], smask, sp_t[:, :], 0.0, ALU.mult, ALU.add),
                 [b_sp, b_cst], [b_bs])
            s.op("act", lambda e: e.activation(e1[:, :], bs_t[:, :], AF.Exp, scale=-1.0 / 16), [b_bs], [b_e1])
            s.op("act", lambda e: e.activation(e2[:, :], bs_t[:, :], AF.Exp, scale=1.0 / 16), [b_bs], [b_e2])
            e1c = e1[:, :].rearrange("p (c j) -> p c j", j=CH)
            s.op("dve", lambda e: e.tensor_copy(glast[:, dt, :], e1c[:, :, CH - 1]), [b_e1], [b_gl])
            s.op("dve", lambda e: e.tensor_tensor_scan(pt[:, 1:NCH + 1], glast[:, dt, :], zer[:, :], 1.0, ALU.mult, ALU.add),
                 [b_gl, b_pt], [b_pt])
            yield

            def ev_q(tt, ps, bps):
                s.op("dve", lambda e: e.scalar_tensor_tensor(bs_t[:, tsl(tt)], ps, QSCALE, e1[:, tsl(tt)], ALU.mult, ALU.mult),
                     [bps, b_e1], [b_bs])
            yield from proj(wq, ev_q)
            while pending_tr:
                pending_tr.pop(0)()
            if dt == 0 and inject is not None:
                inject()
            s.op("act", lambda e: e.copy(qd[:, dt, :], bs_t[:, :]), [b_bs], [b_qd])
            s.op("dve", lambda e: e.tensor_tensor(qg[:, dt, :].rearrange("p (c j) -> p c j", j=CH),
                                                  bs_t[:, :].rearrange("p (c j) -> p c j", j=CH),
                                                  pt[:, 0:NCH].unsqueeze(2).to_broadcast([128, NCH, CH]), ALU.mult),
                 [b_bs, b_pt], [b_qg])

            def ev_k(tt, ps, bps):
                s.op("dve", lambda e: e.tensor_tensor(sp_t[:, tsl(tt)], ps, e2[:, tsl(tt)], ALU.mult), [bps, b_e2], [b_sp])
            wk = load_w(hd, 2 * dt + 1)
            yield from proj(wk, ev_k)
            s.op("act", lambda e: e.copy(ki[:, dt, :], sp_t[:, :]), [b_sp], [b_ki])
            e1l = e1c[:, :, CH - 1:CH].to_broadcast([128, NCH, CH])
            s.op("dve", lambda e: e.tensor_tensor(kdT[:, :].rearrange("p (c j) -> p c j", j=CH),
                                                  sp_t[:, :].rearrange("p (c j) -> p c j", j=CH), e1l, ALU.mult),
                 [b_sp, b_e1], [b_kdT])
            def tr(dt=dt):
                for g in range(3):
                    mb = mbank()
                    pv = ps_m[mb][:, :].bitcast(BF16)
                    for j in range(4):
                        c = g * 4 + j
                        s.op("pe", lambda e: e.transpose(pv[0:CH, j * 128:(j + 1) * 128], kdT[:, c * CH:(c + 1) * CH], ident[:, :]),
                             [b_kdT, b_cst], [b_m[mb]])
                    s.op("act", lambda e: e.copy(kdt[0:CH, g * 4:(g + 1) * 4, dsl(dt)],
                                                 pv[0:CH, 0:512].rearrange("p (c v) -> p c v", v=128)),
                         [b_m[mb]], [b_kdt])
            pending_tr.append(tr)
            yield
        while pending_tr:
            pending_tr.pop(0)()
        for g in range(3):
            mb = mbank()
            for j in range(4):
                c = g * 4 + j
                cs = slice(c * CH, (c + 1) * CH)
                for dt in range(ND):
                    s.op("pe", lambda e: e.matmul(ps_m[mb][0:CH, j * 128:j * 128 + CH], ki[:, dt, cs], qd[:, dt, cs],
                                                  start=(dt == 0), stop=(dt == ND - 1)), [b_ki, b_qd], [b_m[mb]])
            s.op("dve", lambda e: e.tensor_tensor(a_sb[0:CH, g * 4:(g + 1) * 4, :],
                                                  ps_m[mb][0:CH, :].rearrange("p (c v) -> p c v", v=128)[:, :, 0:CH],
                                                  cmask.unsqueeze(1).to_broadcast([CH, 4, CH]), ALU.mult),
                 [b_m[mb], b_cst], [b_asb])
        yield

    def stageB(hd, v):
        par = v % 2
        wv = load_w(hd, 8 + 2 * v)

        def ev_v(tt, ps, bps):
            s.op("act", lambda e: e.copy(vT[:, tsl(tt)], ps), [bps], [b_vT])
        yield from proj(wv, ev_v)
        wz = load_w(hd, 8 + 2 * v + 1)

        def ev_z(tt, ps, bps):
            s.op("act", lambda e: e.activation(sz[v % 3][:, tsl(tt)], ps, AF.Silu), [bps], [b_sz[v % 3]])
        yield from proj(wz, ev_z)
        for g in range(3):
            mb = mbank()
            pv = ps_m[mb][:, :].bitcast(BF16)
            for j in range(4):
                c = g * 4 + j
                s.op("pe", lambda e: e.transpose(pv[0:CH, j * 128:(j + 1) * 128], vT[:, c * CH:(c + 1) * CH], ident[:, :]),
                     [b_vT, b_cst], [b_m[mb]])
            s.op("dve", lambda e: e.tensor_copy(vt_[par][0:CH, g * 4:(g + 1) * 4, :],
                                                pv[0:CH, 0:512].rearrange("p (c v) -> p c v", v=128)),
                 [b_m[mb]], [b_vt[par]])

    def finalize(hd, v):
        par = v % 2
        cc = hd * NV + v
        X.recv(Sp_f[par][:, :], b_Spf[par], Sp_b[par][:, :], b_Spb[par], flag[:, 0:1], b_flag)
        for tt in range(NTT):
            mb = mbank()
            for dt in range(ND):
                s.op("pe", lambda e: e.matmul(ps_m[mb][:, 0:TT], Sp_b[par][:, dsl(dt)], qg[:, dt, tsl(tt)],
                                              start=(dt == 0), stop=(dt == ND - 1)), [b_Spb[par], b_qg], [b_m[mb]])
            s.op("dve", lambda e: e.tensor_tensor(o_f[par][:, tsl(tt)], o_f[par][:, tsl(tt)], ps_m[mb][:, 0:TT], ALU.add),
                 [b_of[par], b_m[mb]], [b_of[par]])
            s.op("act", lambda e: e.activation(sq[tt][:, :], o_f[par][:, tsl(tt)], AF.Square), [b_of[par]], [b_sq[tt]])
            yield
        for tt in range(NTT):
            ob = tt % 2
            mb = mbank()
            s.op("pe", lambda e: e.matmul(ps_m[mb][:, 0:TT], ones[:, :], sq[tt][:, :], start=True, stop=True),
                 [b_sq[tt], b_cst], [b_m[mb]])
            if v == 0:
                s.op("dve", lambda e: e.tensor_copy(ssacc[:, tsl(tt)], ps_m[mb][:, 0:TT]), [b_m[mb]], [b_ss])
            else:
                s.op("dve", lambda e: e.tensor_tensor(ssacc[:, tsl(tt)], ssacc[:, tsl(tt)], ps_m[mb][:, 0:TT], ALU.add),
                     [b_m[mb], b_ss], [b_ss])
            s.op("dve", lambda e: e.scalar_tensor_tensor(yu_t[ob][:, :], o_f[par][:, tsl(tt)], vec[:, cc:cc + 1], sz[v % 3][:, tsl(tt)],
                                                         ALU.mult, ALU.mult), [b_of[par], b_sz[v % 3], b_vec], [b_yu[ob]])
            s.dma("sp", yu_d[cc][:, tsl(tt)], yu_t[ob][:, :], reads=[b_yu[ob]], writes=[b_yud])
            yield

    def stageC(hd, v):
        par = v % 2
        s.op("dve", lambda e: e.memset(S_f[:, :], 0.0), [], b_Sf)
        s.op("dve", lambda e: e.memset(S_b[0][:, :], 0.0), [], b_Sb[0])
        for tt in range(NTT):
            ob = tt % 2
            for j in range(CPT):
                c = tt * CPT + j
                cs = slice(c * CH, (c + 1) * CH)
                sb_ = c % 2
                sbi = c % 2
                for dt in range(ND):
                    s.op("pe", lambda e: e.matmul(ps_sd[sb_][:, dsl(dt)], kdt[0:CH, c, dsl(dt)],
                                                  vt_[par][0:CH, c, :], start=True, stop=True), [b_kdt, b_vt[par]], [b_sd[sb_]])
                osl = ps_o[ob][:, j * CH:(j + 1) * CH]
                s.op("pe", lambda e: e.matmul(osl, vt_[par][0:CH, c, :], a_sb[0:CH, c, :], start=True, stop=False),
                     [b_vt[par], b_asb], [b_o[ob]])
                for dt in range(ND):
                    s.op("pe", lambda e: e.matmul(osl, S_b[sbi][:, dsl(dt)], qd[:, dt, cs], start=False, stop=(dt == ND - 1)),
                         [b_Sb[sbi][dt], b_qd], [b_o[ob]])
                if c < NCH - 1:
                    for dt in range(ND):
                        s.op("dve", lambda e: e.scalar_tensor_tensor(S_b[1 - sbi][:, dsl(dt)], S_f[:, dsl(dt)], glast[:, dt, c:c + 1],
                                                                     ps_sd[sb_][:, dsl(dt)], ALU.mult, ALU.add),
                             [b_Sf[dt], b_sd[sb_], b_gl], [b_Sb[1 - sbi][dt]])
                for dt in range(ND):
                    s.op("dve", lambda e: e.scalar_tensor_tensor(S_f[:, dsl(dt)], S_f[:, dsl(dt)], glast[:, dt, c:c + 1],
                                                                 ps_sd[sb_][:, dsl(dt)], ALU.mult, ALU.add),
                         [b_Sf[dt], b_sd[sb_], b_gl], [b_Sf[dt]])
                yield
            s.op("act", lambda e: e.copy(o_f[par][:, tsl(tt)], ps_o[ob][:, 0:TT]), [b_o[ob]], [b_of[par]])
            if tt == 1 and v > 0:
                yield from finalize(hd, v - 1)
            yield
        X.send(S_f[:, :], b_Sf, Sp_f[par][:, :], b_Spf[par])

    def head_end(hd):
        s.op("act", lambda e: e.activation(ssacc[:, :], ssacc[:, :], AF.Ln, bias=eps_c[:, 0:1], scale=1.0), [b_ss, b_cst], [b_ss])
        s.op("act", lambda e: e.activation(ssacc[:, :], ssacc[:, :], AF.Exp, scale=-0.5), [b_ss], [b_ss])
        s.dma("sp", rs_d[hd], ssacc[:, :], reads=[b_ss], writes=[b_yud])

    def tail(hd):
        for _ in finalize(hd, NV - 1):
            pass
        head_end(hd)

    for hd in range(H):
        for _ in stageA(hd, inject=(lambda p=hd - 1: tail(p)) if hd > 0 else None):
            pass
        for _ in stageB(hd, 0):
            pass
        for v in range(NV):
            if v + 1 < NV:
                interleave(stageB(hd, v + 1), stageC(hd, v), 1.12)
            else:
                for _ in stageC(hd, v):
                    pass
    tail(H - 1)
    if not do_out:
        return
    P.new_phase()
    out_ln_phase(P, C, None, w_out, lnv, xT, rT_d, hT_out, None, yu=(yu_d, rs_d, b_yud))


def out_ln_phase(P, C, yT_d, w_out, lnv, resT, rT_d, hT_out, b_yTd, yu=None):
    nc, s = P.nc, P.s
    cst_f, b_cst = C["cst_f"], C["b_cst"]
    ps_acc, b_acc = P.ps_proj, P.b_proj
    ps_st = [P.ps_m[0], P.ps_m[1], P.ps_o[0], P.ps_o[1], P.ps_sd[0], P.ps_sd[1]]
    b_st = [P.b_m[0], P.b_m[1], P.b_o[0], P.b_o[1], P.b_sd[0], P.b_sd[1]]
    lnp = P.sb("lnp", [128, 2, KC], F32); b_lnp = Buf("lnp")
    mean = P.sb("mean", [128, NTOK], F32); b_mean = Buf("mean")
    rstd = P.sb("rstdl", [128, NTOK], F32); b_rstd = Buf("rstdl")
    lneps = P.sb("lneps", [128, 1], F32)
    onesd = P.sb("onesd", [128, 128], BF16)
    res = [P.sb(f"res{i}", [128, NTOK], F32) for i in range(2)]; b_res = [Buf(f"res{i}") for i in range(2)]
    rr = [P.sb(f"rr{i}", [128, NTOK], F32) for i in range(2)]; b_rr = [Buf(f"rr{i}") for i in range(2)]
    s.op("dve", lambda e: e.tensor_scalar(onesd[:, :], cst_f[:, 128:256], 128.0 / D, None, ALU.mult), [b_cst], [b_cst])
    s.op("dve", lambda e: e.memset(lneps[:, :], LN_EPS), [], [b_rstd])
    s.dma("sp", lnp[:, :, :], lnv, writes=[b_lnp])
    outer_stack = P.stack
    P.stack = ExitStack()
    yT = P.sb("yT", [128, 64, NTOK], BF16)
    b_yT = [Buf(f"yT{c}") for c in range(64)]
    if yu is None:
        for c in range(64):
            s.dma("sp", yT[:, c, :], yT_d[c], reads=[b_yTd], writes=[b_yT[c]])
    else:
        yu_d, rs_d, b_yud = yu
        stg = res + [P.sb(f"yst{i}", [128, NTOK], F32) for i in range(2)]
        b_stg = b_res + [Buf("yst0"), Buf("yst1")]
        for c in range(64):
            hd = c // 8
            i = c % 4
            if c % 8 == 0:
                s.dma("sp", rr[hd % 2][:, :], rs_d[hd], reads=[b_yud], writes=[b_rr[hd % 2]])
            s.dma("sp", stg[i][:, :], yu_d[c], reads=[b_yud], writes=[b_stg[i]])
            s.op("dve", lambda e: e.tensor_tensor(yT[:, c, :], stg[i][:, :], rr[hd % 2][:, :], ALU.mult),
                 [b_stg[i], b_rr[hd % 2]], [b_yT[c]])
    NWO = 2
    wo = [P.sb(f"wo{i}", [128, 64, 128], BF16) for i in range(NWO)]; b_wo = [Buf(f"wo{i}") for i in range(NWO)]
    rhi = [P.sb(f"rhi{i}", [128, TT], BF16) for i in range(2)]; b_rhi = [Buf(f"rhi{i}") for i in range(2)]
    rlo = [P.sb(f"rlo{i}", [128, TT], BF16) for i in range(2)]; b_rlo = [Buf(f"rlo{i}") for i in range(2)]
    rsq = [P.sb(f"rsq{i}", [128, TT], BF16) for i in range(2)]; b_rsq = [Buf(f"rsq{i}") for i in range(2)]
    b_rTd = [Buf(f"rTd{k}") for k in range(KC)]
    pc = 0
    WSO = WStream(P, [w_out[dm] for dm in range(KC)], wo, b_wo)
    pending = []

    def flush_stats():
        while pending:
            tt_, si_, first, last = pending.pop(0)
            s.op("pe", lambda e: e.matmul(ps_st[tt_][:, 0:TT], onesd[:, :], rhi[si_][:, :], start=first, stop=False),
                 [b_rhi[si_], b_cst], [b_st[tt_]])
            s.op("pe", lambda e: e.matmul(ps_st[tt_][:, 0:TT], onesd[:, :], rlo[si_][:, :], start=False, stop=last),
                 [b_rlo[si_], b_cst], [b_st[tt_]])
            s.op("pe", lambda e: e.matmul(ps_st[3 + tt_][:, 0:TT], onesd[:, :], rsq[si_][:, :], start=first, stop=last),
                 [b_rsq[si_], b_cst], [b_st[3 + tt_]])

    for dm in range(KC):
        wi = WSO.get(dm)
        ri = dm % 2
        s.dma("sp", res[ri][:, :], resT[dm], writes=[b_res[ri]])
        for tt in range(NTT):
            pb = P.pbank()
            pc += 1
            ts = slice(tt * TT, (tt + 1) * TT)
            for c in range(64):
                s.op("pe", lambda e: e.matmul(ps_acc[pb][:, 0:TT], wo[wi][:, c, :], yT[:, c, ts], start=(c == 0), stop=(c == 63)),
                     [b_wo[wi], b_yT[c]], [b_acc[pb]])
            flush_stats()
            si = pc % 2
            s.op("dve", lambda e: e.scalar_tensor_tensor(rr[ri][:, ts], res[ri][:, ts], ALPHA, ps_acc[pb][:, 0:TT], ALU.mult, ALU.add),
                 [b_res[ri], b_acc[pb]], [b_rr[ri]])
            s.op("act", lambda e: e.copy(rhi[si][:, :], rr[ri][:, ts]), [b_rr[ri]], [b_rhi[si]])
            s.op("dve", lambda e: e.tensor_tensor(rlo[si][:, :], rr[ri][:, ts], rhi[si][:, :], ALU.subtract),
                 [b_rr[ri], b_rhi[si]], [b_rlo[si]])
            s.op("act", lambda e: e.activation(rsq[si][:, :], rr[ri][:, ts], AF.Square), [b_rr[ri]], [b_rsq[si]])
            pending.append((tt, si, dm == 0, dm == KC - 1))
        s.dma("sp", rT_d[dm], rr[ri][:, :], reads=[b_rr[ri]], writes=[b_rTd[dm]])
    flush_stats()
    for tt in range(NTT):
        ts = slice(tt * TT, (tt + 1) * TT)
        s.op("act", lambda e: e.copy(mean[:, ts], ps_st[tt][:, 0:TT]), [b_st[tt]], [b_mean])
        s.op("dve", lambda e: e.tensor_tensor(rstd[:, ts], mean[:, ts], mean[:, ts], ALU.mult), [b_mean], [b_rstd])
        s.op("dve", lambda e: e.tensor_tensor(rstd[:, ts], ps_st[3 + tt][:, 0:TT], rstd[:, ts], ALU.subtract), [b_st[3 + tt], b_rstd], [b_rstd])
    s.op("act", lambda e: e.activation(rstd[:, :], rstd[:, :], AF.Ln, bias=lneps[:, 0:1], scale=1.0), [b_rstd], [b_rstd])
    s.op("act", lambda e: e.activation(rstd[:, :], rstd[:, :], AF.Exp, scale=-0.5), [b_rstd], [b_rstd])
    b_out = Buf("hT_out")
    s.barrier()
    P.stack.close()
    P.stack = outer_stack
    for i in range(2, 4):
        res.append(P.sb(f"res{i}", [128, NTOK], F32)); b_res.append(Buf(f"res{i}"))
        rr.append(P.sb(f"rr{i}", [128, NTOK], F32)); b_rr.append(Buf(f"rr{i}"))
    for kc in range(KC):
        ri = kc % 4
        s.dma("sp", rr[ri][:, :], rT_d[kc], reads=[b_rTd[kc]], writes=[b_rr[ri]])
        s.op("dve", lambda e: e.tensor_tensor(res[ri][:, :], rr[ri][:, :], mean[:, :], ALU.subtract), [b_rr[ri], b_mean], [b_res[ri]])
        s.op("dve", lambda e: e.tensor_tensor(res[ri][:, :], res[ri][:, :], rstd[:, :], ALU.mult), [b_res[ri], b_rstd], [b_res[ri]])
        s.op("act", lambda e: e.activation(res[ri][:, :], res[ri][:, :], AF.Identity, bias=lnp[:, 1, kc:kc + 1], scale=lnp[:, 0, kc:kc + 1]),
             [b_res[ri], b_lnp], [b_res[ri]])
        s.dma("pool", hT_out[kc], res[ri][:, :], reads=[b_res[ri]], writes=[b_out])
    s.drain_dma("sp")
    s.drain_dma("pool")
    return b_out


def build_fused(H0=64, H1=8, groups=None, xmode="cc", layers=(0, 1), do_out=True, debug=False):
    P = Prog()
    nc, s = P.nc, P.s
    groups = groups or [[0, 1], [2, 3], [4, 5], [6, 7]]
    cst = P.din("cst", [128, 128 + 128 + CH + NTOK])
    flag_d = P.din("flag", [128, 1])
    cst_f = P.sb("cst_f", [128, 128 + 128 + CH + NTOK], F32, persist=True)
    ident = P.sb("ident", [128, 128], BF16, persist=True)
    ones128 = P.sb("ones128", [128, 128], BF16, persist=True)
    ones1024 = P.sb("ones1024", [128, 128], BF16, persist=True)
    eps_c = P.sb("eps_c", [128, 1], F32, persist=True)
    flag = P.sb("flag_sb", [128, 1], F32, persist=True)
    b_cst, b_flag = Buf("cst"), Buf("flag")
    s.dma("sp", cst_f[:, :], cst, writes=[b_cst])
    s.dma("sp", flag[:, :], flag_d, writes=[b_flag])
    s.op("dve", lambda e: e.memset(eps_c[:, :], RMS_EPS), [], [b_cst])
    s.op("dve", lambda e: e.tensor_copy(ident[:, :], cst_f[:, 0:128]), [b_cst], [b_cst])
    s.op("dve", lambda e: e.tensor_copy(ones128[:, :], cst_f[:, 128:256]), [b_cst], [b_cst])
    s.op("dve", lambda e: e.tensor_scalar(ones1024[:, :], cst_f[:, 128:256], 128.0 / 1024, None, ALU.mult), [b_cst], [b_cst])
    C = dict(cst_f=cst_f, ident=ident, ones128=ones128, ones1024=ones1024, eps_c=eps_c, b_cst=b_cst, flag=flag, b_flag=b_flag)
    dbg = P.dout if debug else P.dscr
    xT = P.din("xT", [KC, 128, NTOK])
    b_h1 = Buf("h1T")
    if 0 in layers:
        w_in0 = P.din("w_in0", [H0, 4, 128, KC, 128])
        vecs0 = P.din("vecs0", [128, 5, 64])
        yT_d = dbg("yT_d", [64, 128, NTOK], BF16)
        if do_out:
            w_out0 = P.din("w_out0", [KC, 128, 64, 128])
            lnv0 = P.din("lnv0", [128, 2, KC])
            rT_d = P.dscr("rT_d", [KC, 128, NTOK], F32)
            h1T = (P.dout if (debug or 1 not in layers) else P.dscr)("h1T", [KC, 128, NTOK], F32)
        else:
            w_out0 = lnv0 = rT_d = h1T = None
        emit_layer0(P, C, xT, w_in0, vecs0, w_out0, lnv0, yT_d, rT_d, h1T, groups, xmode, H=H0, do_out=do_out)
        if do_out:
            P.new_phase()
        x1 = h1T
    else:
        x1 = xT
    if 1 in layers:
        w_in1 = P.din("w_in1", [H1, 24, 128, KC, 128])
        wg1 = P.din("wg1", [128, KC, 16])
        wg2 = P.din("wg2", [16, 4096])
        vecs1 = P.din("vecs1", [128, 96])
        yu_d = dbg("yu_d", [64, 128, NTOK], F32)
        rs_d = dbg("rs_d", [8, 128, NTOK], F32)
        if do_out:
            w_out1 = P.din("w_out1", [KC, 128, 64, 128])
            lnv1 = P.din("lnv1", [128, 2, KC])
            rT_d1 = P.dscr("rT_d1", [KC, 128, NTOK], F32)
            outT = P.dout("outT", [KC, 128, NTOK], F32)
        else:
            w_out1 = lnv1 = rT_d1 = outT = None
        emit_layer1(P, C, x1, b_h1, w_in1, wg1, wg2, vecs1, w_out1, lnv1, yu_d, rs_d, rT_d1, outT, groups, xmode, H=H1, do_out=do_out)
    s.barrier()
    return P


def _relayout_w_in0(w):
    return np.ascontiguousarray(w.reshape(32, 128, 4, 64, 128).transpose(3, 2, 1, 0, 4))


def _relayout_w_out(w):
    return np.ascontiguousarray(w.reshape(64, 128, 32, 128).transpose(2, 1, 0, 3))


def _relayout_w_in1(w):
    wr = w.reshape(32, 128, 24576)
    out = np.empty((8, 24, 128, 32, 128), np.float32)
    for hd in range(8):
        for dt in range(4):
            out[hd, 2 * dt] = wr[:, :, hd * 512 + dt * 128: hd * 512 + (dt + 1) * 128].transpose(1, 0, 2)
            out[hd, 2 * dt + 1] = wr[:, :, 4096 + hd * 512 + dt * 128: 4096 + hd * 512 + (dt + 1) * 128].transpose(1, 0, 2)
        for v in range(8):
            out[hd, 8 + 2 * v] = wr[:, :, 8192 + hd * 1024 + v * 128: 8192 + hd * 1024 + (v + 1) * 128].transpose(1, 0, 2)
            out[hd, 9 + 2 * v] = wr[:, :, 16384 + hd * 1024 + v * 128: 16384 + hd * 1024 + (v + 1) * 128].transpose(1, 0, 2)
    return out


def _make_cst():
    cst = np.zeros((128, 256 + CH + NTOK), np.float32)
    cst[:, 0:128] = np.eye(128, dtype=np.float32)
    cst[:, 128:256] = 1.0 / 128
    jj = np.arange(CH)
    cst[0:CH, 256:256 + CH] = (jj[:, None] <= jj[None, :]).astype(np.float32)
    sm = np.ones(NTOK, np.float32)
    sm[::CH] = 0
    cst[:, 256 + CH:] = sm[None, :]
    return cst


_PROG = {}


def kernel(x, meta, lb_logits, l0_w_in, l0_b_f, l0_norm_g, l0_w_out, l0_ln_g, l0_ln_b,
           l1_w_in, l1_w_g1, l1_w_g2, l1_b_g, l1_norm_g, l1_w_out, l1_ln_g, l1_ln_b):
    f = lambda a: np.ascontiguousarray(np.asarray(a, dtype=np.float32))
    x, meta = f(x), f(meta)
    B = x.shape[0]
    h0 = np.concatenate([np.broadcast_to(meta[None], (B, 16, D)), x], axis=1)
    lbl = f(lb_logits)
    vecs0 = np.stack([f(l0_b_f), f(l0_norm_g), lbl[0], lbl[1], lbl[2]], 0)
    shared = {
        "cst": _make_cst(),
        "w_in0": _relayout_w_in0(f(l0_w_in)),
        "vecs0": np.ascontiguousarray(vecs0.reshape(5, 64, 128).transpose(2, 0, 1)),
        "w_out0": _relayout_w_out(f(l0_w_out)),
        "lnv0": np.ascontiguousarray(np.stack([f(l0_ln_g), f(l0_ln_b)], 0).reshape(2, 32, 128).transpose(2, 0, 1)),
        "w_in1": _relayout_w_in1(f(l1_w_in)),
        "wg1": np.ascontiguousarray(f(l1_w_g1).reshape(32, 128, 16).transpose(1, 0, 2)),
        "wg2": f(l1_w_g2),
        "vecs1": np.ascontiguousarray(np.concatenate([f(l1_norm_g).reshape(64, 128).T, f(l1_b_g).reshape(32, 128).T], 1)),
        "w_out1": _relayout_w_out(f(l1_w_out)),
        "lnv1": np.ascontiguousarray(np.stack([f(l1_ln_g), f(l1_ln_b)], 0).reshape(2, 32, 128).transpose(2, 0, 1)),
    }
    in_maps = []
    for c in range(NCORES):
        b, half = c // 2, c % 2
        d = dict(shared)
        d["xT"] = np.ascontiguousarray(h0[b, half * NTOK:(half + 1) * NTOK].T).reshape(KC, 128, NTOK)
        d["flag"] = np.full((128, 1), float(half), np.float32)
        in_maps.append(d)
    if "p" not in _PROG:
        _PROG["p"] = build_fused()
    res = run_bass_kernel_spmd(_PROG["p"].nc, in_maps, core_ids=list(range(NCORES)))
    out = np.empty((B, 2064, D), np.float32)
    for c in range(NCORES):
        b, half = c // 2, c % 2
        out[b, half * NTOK:(half + 1) * NTOK] = res.results[c]["outT"].reshape(D, NTOK).T
    return np.ascontiguousarray(out[:, 16:])
```
